# Optimizing a Trainium2 kernel written in Bass

```python
import math
import jax, jax.numpy as jnp
from jax import lax
import numpy as np

D_MODEL = 1024
BATCH = 8
SEQ = 2048
DEPTH = 1

CHUNK = 64
SSD_HEADS = 16
SSD_HEAD_DIM = 64
SSD_INNER = SSD_HEADS * SSD_HEAD_DIM
SSD_GROUPS = 2
SSD_STATE = 128
SSD_CONV = 4
SSD_SCAN_CHUNK = CHUNK
CONV_DIM = SSD_INNER + 2 * SSD_GROUPS * SSD_STATE
SB_HEADS = 16
SB_HEAD_DIM = 64
SB_INNER = SB_HEADS * SB_HEAD_DIM
SB_BLOCK = 128
N_BRANCH = 2
IN_SIZES = (SSD_INNER, CONV_DIM, SSD_HEADS, SB_INNER, SB_INNER, SB_INNER, SB_INNER, N_BRANCH * D_MODEL)
IN_COLS = sum(IN_SIZES)
EPS = 1e-6

kernel_name = "hybrid_ssd_stickbreaking_gated_block"


def _split_points():
    pts, acc = [], 0
    for s in IN_SIZES[:-1]:
        acc += s
        pts.append(acc)
    return pts


def rmsnorm(x, w):
    xf = x.astype(jnp.float32)
    y = xf * lax.rsqrt(jnp.mean(xf * xf, axis=-1, keepdims=True) + EPS)
    return (y * w.astype(jnp.float32)).astype(x.dtype)


def gated_group_rmsnorm(y, z, w, groups):
    g = (y.astype(jnp.float32) * jax.nn.silu(z.astype(jnp.float32)))
    g = g.reshape(y.shape[:-1] + (groups, y.shape[-1] // groups))
    g = g * lax.rsqrt(jnp.mean(g * g, axis=-1, keepdims=True) + EPS)
    return (g.reshape(y.shape) * w.astype(jnp.float32)).astype(y.dtype)


def ssd_scan(xh, dt, a, bmat, cmat, d_skip):
    bsz, L, H, P = xh.shape
    G, N = bmat.shape[2], bmat.shape[3]
    R = H // G
    T = SSD_SCAN_CHUNK
    NC = L // T
    x_dt = (xh * dt[..., None]).reshape(bsz, NC, T, G, R, P)
    a_dt = jnp.transpose((dt * a).reshape(bsz, NC, T, G, R), (0, 3, 4, 1, 2))
    a_cs = jnp.cumsum(a_dt, axis=-1)
    bc = bmat.reshape(bsz, NC, T, G, N)
    cc = cmat.reshape(bsz, NC, T, G, N)
    seg = a_cs[..., :, None] - a_cs[..., None, :]
    causal = jnp.tril(jnp.ones((T, T), dtype=bool))
    decay = jnp.exp(jnp.where(causal, seg, -jnp.inf))
    scores = jnp.einsum('bclgn,bcsgn->bgcls', cc, bc)
    attn = scores[:, :, None] * decay
    y_diag = jnp.einsum('bgrcls,bcsgrp->bclgrp', attn, x_dt)
    decay_states = jnp.exp(a_cs[..., -1:] - a_cs)
    states = jnp.einsum('bclgn,bgrcl,bclgrp->bcgrpn', bc, decay_states, x_dt)
    chunk_decay = jnp.exp(a_cs[..., -1])

    def step(carry, inp):
        st, dec = inp
        new = carry * dec[..., None, None] + st
        return new, carry

    init = jnp.zeros((bsz, G, R, P, N), dtype=xh.dtype)
    _, prev = lax.scan(step, init, (jnp.moveaxis(states, 1, 0), jnp.moveaxis(chunk_decay, -1, 0)))
    prev = jnp.moveaxis(prev, 0, 1)
    y_off = jnp.einsum('bclgn,bcgrpn,bgrcl->bclgrp', cc, prev, jnp.exp(a_cs))
    y = (y_diag + y_off).reshape(bsz, L, H, P)
    return y + xh * d_skip[:, None]


def stick_breaking(q, k, v):
    bsz, L, H, Dh = q.shape
    scale = Dh ** -0.5
    outs = []
    for blk in range(L // SB_BLOCK):
        q0 = blk * SB_BLOCK
        kend = q0 + SB_BLOCK
        logits = jnp.einsum('bqhd,bkhd->bhqk', q[:, q0:kend], k[:, :kend]).astype(jnp.float32) * scale
        qi = q0 + jnp.arange(SB_BLOCK)[:, None]
        ki = jnp.arange(kend)[None, :]
        mask = ki < qi
        log_fail = jnp.where(mask, jax.nn.log_sigmoid(-logits), 0.0)
        between = lax.cumsum(log_fail, axis=3, reverse=True) - log_fail
        w = jnp.where(mask, jnp.exp(jax.nn.log_sigmoid(logits) + between), 0.0)
        outs.append(jnp.einsum('bhqk,bkhd->bqhd', w.astype(v.dtype), v[:, :kend]))
    return jnp.concatenate(outs, axis=1)


def hybrid_layer(x, norm_pre, w_in, b_gate, conv_w, conv_b, dt_bias, a_log, d_skip,
                 ssd_norm, w_ssd_proj, w_sb_proj, w_out, norm_post):
    bsz, L, _ = x.shape
    h = rmsnorm(x, norm_pre)
    proj = jnp.einsum('bld,de->ble', h, w_in)
    z_ssd, xbc, dt_raw, q, k, v, z_sb, gates = jnp.split(proj, _split_points(), axis=-1)

    xbc = lax.conv_general_dilated(
        xbc, conv_w[:, None, :], window_strides=(1,), padding=[(SSD_CONV - 1, 0)],
        dimension_numbers=('NWC', 'WIO', 'NWC'), feature_group_count=CONV_DIM) + conv_b
    xbc = jax.nn.silu(xbc)
    xs, bm, cm = jnp.split(xbc, [SSD_INNER, SSD_INNER + SSD_GROUPS * SSD_STATE], axis=-1)
    dt = jax.nn.softplus(dt_raw.astype(jnp.float32) + dt_bias.astype(jnp.float32))
    a = -jnp.exp(a_log.astype(jnp.float32))
    y = ssd_scan(xs.reshape(bsz, L, SSD_HEADS, SSD_HEAD_DIM).astype(jnp.float32), dt, a,
                 bm.reshape(bsz, L, SSD_GROUPS, SSD_STATE).astype(jnp.float32),
                 cm.reshape(bsz, L, SSD_GROUPS, SSD_STATE).astype(jnp.float32),
                 d_skip.astype(jnp.float32))
    y = y.reshape(bsz, L, SSD_INNER).astype(x.dtype)
    y = gated_group_rmsnorm(y, z_ssd, ssd_norm, SSD_GROUPS)
    y_ssd = jnp.einsum('ble,ed->bld', y, w_ssd_proj)

    o = stick_breaking(q.reshape(bsz, L, SB_HEADS, SB_HEAD_DIM),
                       k.reshape(bsz, L, SB_HEADS, SB_HEAD_DIM),
                       v.reshape(bsz, L, SB_HEADS, SB_HEAD_DIM)).reshape(bsz, L, SB_INNER)
    y_sb = jnp.einsum('ble,ed->bld', o * jax.nn.silu(z_sb), w_sb_proj)

    g = jax.nn.sigmoid(gates + b_gate)
    g_ssd, g_sb = jnp.split(g, [D_MODEL], axis=-1)
    merged = g_ssd * y_ssd + g_sb * y_sb
    out = jnp.einsum('bld,de->ble', merged, w_out)
    return x + rmsnorm(out, norm_post)


def setup_inputs(seed: int = 0) -> dict:
    key = jax.random.key(seed)
    ks = jax.random.split(key, 16)
    f32 = jnp.float32
    x = jax.random.normal(ks[0], (BATCH, SEQ, D_MODEL), f32)
    norm_pre = 1.0 + 0.05 * jax.random.normal(ks[1], (DEPTH, D_MODEL), f32)
    w_in = jax.random.normal(ks[2], (DEPTH, D_MODEL, IN_COLS), f32) * D_MODEL ** -0.5
    b_gate = 0.02 * jax.random.normal(ks[3], (DEPTH, N_BRANCH * D_MODEL), f32)
    conv_w = jax.random.normal(ks[4], (DEPTH, SSD_CONV, CONV_DIM), f32) * SSD_CONV ** -0.5
    conv_b = 0.02 * jax.random.normal(ks[5], (DEPTH, CONV_DIM), f32)
    u = jax.random.uniform(ks[6], (DEPTH, SSD_HEADS), f32)
    dt0 = jnp.exp(u * (math.log(0.1) - math.log(0.001)) + math.log(0.001))
    dt_bias = dt0 + jnp.log(-jnp.expm1(-dt0))
    a_log = jnp.log(jax.random.uniform(ks[7], (DEPTH, SSD_HEADS), f32, 1.0, 16.0))
    d_skip = 1.0 + 0.05 * jax.random.normal(ks[8], (DEPTH, SSD_HEADS), f32)
    ssd_norm = 1.0 + 0.05 * jax.random.normal(ks[9], (DEPTH, SSD_INNER), f32)
    w_ssd_proj = jax.random.normal(ks[10], (DEPTH, SSD_INNER, D_MODEL), f32) * SSD_INNER ** -0.5
    w_sb_proj = jax.random.normal(ks[11], (DEPTH, SB_INNER, D_MODEL), f32) * SB_INNER ** -0.5
    w_out = jax.random.normal(ks[12], (DEPTH, D_MODEL, D_MODEL), f32) * D_MODEL ** -0.5
    norm_post = 1.0 + 0.05 * jax.random.normal(ks[13], (DEPTH, D_MODEL), f32)
    return {"x": x, "norm_pre": norm_pre, "w_in": w_in, "b_gate": b_gate,
            "conv_w": conv_w, "conv_b": conv_b, "dt_bias": dt_bias, "a_log": a_log,
            "d_skip": d_skip, "ssd_norm": ssd_norm, "w_ssd_proj": w_ssd_proj,
            "w_sb_proj": w_sb_proj, "w_out": w_out, "norm_post": norm_post}


def reference(x, norm_pre, w_in, b_gate, conv_w, conv_b, dt_bias, a_log, d_skip,
              ssd_norm, w_ssd_proj, w_sb_proj, w_out, norm_post):
    for layer in range(DEPTH):
        x = hybrid_layer(x, norm_pre[layer], w_in[layer], b_gate[layer], conv_w[layer],
                         conv_b[layer], dt_bias[layer], a_log[layer], d_skip[layer],
                         ssd_norm[layer], w_ssd_proj[layer], w_sb_proj[layer],
                         w_out[layer], norm_post[layer])
    return x
```

```python
import numpy as np
import concourse.bass as bass
import concourse.mybir as mybir
from concourse.bass_utils import run_bass_kernel_spmd

F32 = mybir.dt.float32
BF16 = mybir.dt.bfloat16
AF = mybir.ActivationFunctionType
ALU = mybir.AluOpType

L = 2048
D = 1024
NT = 16
EPS = 1e-6
NDMA = 24

C_ZSSD, C_XBC, C_DT, C_Q, C_K, C_V, C_ZSB, C_G = 0, 1024, 2560, 2576, 3600, 4624, 5648, 6672


class Op:
    __slots__ = ("eng", "idx", "signal", "semval")

    def __init__(self, eng, idx):
        self.eng = eng
        self.idx = idx
        self.signal = False
        self.semval = None


class Buf:
    __slots__ = ("wr", "rd", "excl")

    def __init__(self, excl=False):
        self.wr = None
        self.rd = {}
        self.excl = excl


class Sched:
    ENG = ["pe", "act", "dve", "pool", "sp"]

    def __init__(self, nc):
        self.nc = nc
        self.prog = {n: [] for n in self.ENG}
        self.cnt = {n: 0 for n in self.ENG}
        self.seen = {n: {} for n in self.ENG}
        self.sem = {n: nc.alloc_semaphore("s_" + n) for n in ["pe", "act", "dve", "pool"]}
        self.dsem = [nc.alloc_semaphore("s_dma%d" % i) for i in range(NDMA)]
        self.dcnt = [0] * NDMA
        self.dlast = [None] * NDMA
        self.drr = 0
        self.drr_sw = 0

    def _deps(self, eng, reads, writes, extra=()):
        deps = {}

        def add(o):
            if o is None:
                return
            if o.eng == eng and eng == "pe":
                return
            cur = deps.get(o.eng)
            if cur is None or o.idx > cur.idx:
                deps[o.eng] = o

        for b in reads:
            add(b.wr)
            if b.excl:
                for o in b.rd.values():
                    if o.eng != eng:
                        add(o)
        for b in writes:
            add(b.wr)
            for o in b.rd.values():
                add(o)
        for o in extra:
            add(o)
        out = []
        for key, o in deps.items():
            if self.seen[eng].get(key, -1) >= o.idx:
                continue
            self.seen[eng][key] = o.idx
            o.signal = True
            out.append(o)
        return out

    def op(self, eng, fn, r=(), w=()):
        waits = self._deps(eng, r, w)
        o = Op(eng, self.cnt[eng])
        self.cnt[eng] += 1
        self.prog[eng].append(("op", waits, fn, o))
        for b in r:
            b.rd[eng] = o
        for b in w:
            b.wr = o
            b.rd = {}
        return o

    def dma(self, q, out, in_, r=(), w=()):
        half = NDMA // 2
        if q == "pool":
            j = half + self.drr_sw % half
            self.drr_sw += 1
        else:
            j = self.drr % half
            self.drr += 1
        key = "dma%d" % j
        extra = [self.dlast[j]] if self.dlast[j] is not None else []
        waits = self._deps(q, r, w, extra)
        o = Op(key, self.dcnt[j])
        self.dcnt[j] += 1
        o.signal = True
        o.semval = 16 * (o.idx + 1)
        self.dlast[j] = o
        self.prog[q].append(("dma", waits, (out, in_, j), o))
        for b in r:
            b.rd[key] = o
        for b in w:
            b.wr = o
            b.rd = {}
        return o

    def wait_ops(self, eng, ops):
        waits = self._deps(eng, (), (), ops)
        self.prog[eng].append(("wait", waits, None, None))

    def fence(self):
        last = []
        for n in ["pe", "act", "dve", "pool"]:
            for kind, waits, fn, o in reversed(self.prog[n]):
                if kind == "op":
                    last.append(o)
                    break
        last += [o for o in self.dlast if o is not None]
        for n in self.ENG:
            self.wait_ops(n, last)

    def _semh(self, o):
        if o.eng.startswith("dma"):
            return self.dsem[int(o.eng[3:])]
        return self.sem[o.eng]

    def emit(self):
        for n in ["pe", "act", "dve", "pool"]:
            c = 0
            for kind, waits, fn, o in self.prog[n]:
                if kind == "op" and o.signal:
                    c += 1
                    o.semval = c
        nc = self.nc

        def run(n, e):
            for kind, waits, fn, o in self.prog[n]:
                for d in waits:
                    e.wait_ge(self._semh(d), d.semval)
                if kind == "op":
                    ins = fn(e)
                    if o.signal:
                        ins.then_inc(self.sem[n], 1)
                elif kind == "dma":
                    out, in_, j = fn
                    e.dma_start(out=out, in_=in_).then_inc(self.dsem[j], 16)

        with nc.Block() as block:
            @block.tensor
            def _(e):
                run("pe", e)

            @block.scalar
            def _(e):
                run("act", e)

            @block.vector
            def _(e):
                run("dve", e)

            @block.gpsimd
            def _(e):
                run("pool", e)

            @block.sync
            def _(e):
                run("sp", e)


def make_consts():
    a = np.arange(128)
    ident = (a[:, None] == a[None, :]).astype(np.float32)
    le = (a[:, None] <= a[None, :]).astype(np.float32)
    gt = (a[:, None] > a[None, :]).astype(np.float32)
    ones = np.ones((128, 128), np.float32)
    negge = -(a[:, None] >= a[None, :]).astype(np.float32)
    negones = -ones
    maskm = np.where(a[:, None] >= a[None, :], -30000.0, 0.0).astype(np.float32)
    return np.concatenate([ident, le, gt, ones, negge, negones, maskm], axis=1)


NCONST = 7
(CI_ID, CI_LE, CI_GT, CI_ONES, CI_NEGGE, CI_NEGONES, CI_MASKM) = range(7)


def build(dbg=None, upto="all"):
    nc = bass.Bass("TRN2", target_bir_lowering=False)
    dbg = dbg or []

    def din(name, shape):
        return nc.dram_tensor(name, list(shape), F32, kind="ExternalInput").ap()

    x_d = din("x", [L, D])
    w_in = din("w_in", [D, 8720])
    wssd_d = din("w_ssd_proj", [D, D])
    wsb_d = din("w_sb_proj", [D, D])
    wout_d = din("w_out", [D, D])
    cst_d = din("cst", [128, NCONST * 128])
    npre_d = din("npre_pc", [128, 8])
    npost_d = din("npost_bc", [128, D])
    ssdn_d = din("ssdn_bc", [128, D])
    bgate_d = din("bgate_pc", [128, 16])
    convw_d = din("convw_pc", [128, 48])
    convb_d = din("convb_pc", [128, 12])
    hv_d = din("headvec_bc", [128, 48])
    y_d = nc.dram_tensor("y", [L, D], F32, kind="ExternalOutput").ap()
    dbg_d = {}
    for name, shape in dbg:
        dbg_d[name] = nc.dram_tensor("dbg_" + name, list(shape), F32, kind="ExternalOutput").ap()

    K = Sched(nc)

    def sb(name, shape, dt=F32):
        return nc.alloc_sbuf_tensor(name, list(shape), dt)

    cf = sb("cf", [128, NCONST * 128], F32)
    cb = sb("cb", [128, NCONST * 128], BF16)
    B_cf, B_cb = Buf(), Buf()

    def CF(i):
        return cf[:, i * 128:(i + 1) * 128]

    def CB(i):
        return cb[:, i * 128:(i + 1) * 128]

    npre = sb("npre", [128, 8])
    npost = sb("npost", [128, D])
    ssdn = sb("ssdn", [128, D])
    bgate = sb("bgate", [128, 16])
    convw = sb("convw", [128, 48])
    convb = sb("convb", [128, 12])
    hv = sb("hv", [128, 48])
    B_par = Buf()
    hT = sb("hT", [128, 8, L], BF16)
    B_hT = [Buf() for _ in range(NT)]
    ps = [nc.alloc_psum_tensor("ps%d" % i, [128, 512], F32) for i in range(8)]
    B_ps = [Buf(excl=True) for _ in range(8)]

    def psbf(i):
        return ps[i][:].bitcast(BF16)

    def load_consts():
        K.dma("pool", cb[:], cst_d, w=[B_cb])
        K.dma("sp", cf[:], cst_d, w=[B_cf])
        ops_ = []
        for t_, d_ in [(npre, npre_d), (npost, npost_d), (ssdn, ssdn_d), (bgate, bgate_d),
                       (convw, convw_d), (convb, convb_d), (hv, hv_d)]:
            ops_.append(K.dma("sp", t_[:], d_))
        B_par.wr = ops_[-1]
        return ops_

    out_ops = []

    wst = [sb("wst%d" % i, [128, 8, 512], BF16) for i in range(2)]
    B_wst = [[Buf() for _ in range(4)] for _ in range(2)]
    wrr = [0]
    w_in_v = w_in.rearrange("(c p) e -> p c e", p=128)

    def load_w(segs, src=None):
        i = wrr[0] % 2
        wrr[0] += 1
        srcv = w_in_v if src is None else src
        off = 0
        ops = []
        for si, (c0, n) in enumerate(segs):
            if si == 0:
                o = K.dma("pool", wst[i][:, :, off:off + n], srcv[:, :, c0:c0 + n], w=B_wst[i])
            else:
                o = K.dma("pool", wst[i][:, :, off:off + n], srcv[:, :, c0:c0 + n])
            ops.append(o)
            off += n
        for s in range(4):
            B_wst[i][s].wr = ops[s] if s < len(ops) else ops[0]
        return wst[i], B_wst[i]

    pre_w = {}

    mA = (nc.sbuf_base, nc.sbuf_top)
    xbuf = [sb("xbuf%d" % i, [128, D]) for i in range(4)]
    B_xbuf = [Buf(), Buf(), Buf(), Buf()]
    hb = [sb("hb%d" % i, [128, D], BF16) for i in range(2)]
    B_hb = [Buf(), Buf()]
    junk = sb("junk", [128, D], BF16)
    B_junk = Buf()
    junkA = sb("junkA", [128, D], BF16)
    B_junkA = Buf()
    stat = sb("stat", [128, 3 * NT])
    B_stat = [Buf() for _ in range(NT)]

    def a_stats(t):
        xt, bx = xbuf[t % 4], B_xbuf[t % 4]
        K.dma("sp", xt[:], x_d[t * 128:(t + 1) * 128, :], w=[bx])
        ss = stat[:, 3 * t:3 * t + 1]
        rms = stat[:, 3 * t + 1:3 * t + 2]
        rstd = stat[:, 3 * t + 2:3 * t + 3]
        if t % 2 == 0:
            K.op("dve", lambda e: e.scalar_tensor_tensor(out=junk[:], in0=xt[:], scalar=1.0, in1=xt[:], op0=ALU.mult, op1=ALU.mult,
                                                         accum_out=ss), r=[bx], w=[B_junk, B_stat[t]])
        else:
            K.op("act", lambda e: e.activation(out=junkA[:], in_=xt[:], func=AF.Square, accum_out=ss),
                 r=[bx], w=[B_junkA, B_stat[t]])
        K.op("act", lambda e: e.activation(out=rms, in_=ss, func=AF.Sqrt, bias=EPS, scale=1.0 / D),
             r=[B_stat[t]], w=[B_stat[t]])
        K.op("dve", lambda e: e.reciprocal(out=rstd, in_=rms), r=[B_stat[t]], w=[B_stat[t]])

    def a_apply(t):
        xt, bx = xbuf[t % 4], B_xbuf[t % 4]
        rstd = stat[:, 3 * t + 2:3 * t + 3]
        h_, bh = hb[t % 2], B_hb[t % 2]
        K.op("act", lambda e: e.activation(out=h_[:], in_=xt[:], func=AF.Copy, scale=rstd), r=[bx, B_stat[t]], w=[bh])
        pi = t % 2
        for c in range(8):
            K.op("pe", lambda e, c=c: e.transpose(out=psbf(pi)[:, c * 128:(c + 1) * 128],
                                                  in_=h_[:, c * 128:(c + 1) * 128], identity=CB(CI_ID)),
                 r=[bh, B_cb], w=[B_ps[pi]])

    def a_evac(t):
        pi = t % 2
        K.op("dve", lambda e: e.tensor_tensor(
            out=hT[:, :, t * 128:(t + 1) * 128],
            in0=psbf(pi).rearrange("p (c l) -> p c l", c=8),
            in1=npre[:, :].unsqueeze(2).to_broadcast([128, 8, 128]), op=ALU.mult),
            r=[B_ps[pi], B_par], w=[B_hT[t]])

    a_stats(0)
    a_stats(1)
    par_ops = load_consts()
    pre_w["c1"] = load_w([(C_XBC, 512)])
    for t in range(-1, NT):
        if 2 <= t + 2 < NT:
            a_stats(t + 2)
        if 0 <= t + 1 < NT:
            a_apply(t + 1)
        if t >= 0:
            if t == 0:
                K.wait_ops("dve", par_ops)
            a_evac(t)

    def mark():
        return (nc.sbuf_base, nc.sbuf_top)

    def release(m):
        nc.sbuf_base, nc.sbuf_top = m

    def finish(dumps):
        for it in dumps:
            if it is None:
                continue
            name, ap, bufs = it
            out_ops.append(K.dma("sp", dbg_d[name], ap, r=bufs))
        K.wait_ops("sp", out_ops)
        K.emit()
        return nc

    def dump_bf(name, src_ap, shape, bufs):
        dst = dbg_d[name]
        if len(shape) == 3:
            dst = dst.rearrange("p (a b) -> p a b", a=shape[1])
        out_ops.append(K.dma("pool", dst, src_ap, r=bufs))
        return None

    if upto == "A":
        return finish([dump_bf("hT", hT[:], [128, 8, L], B_hT)])

    K.fence()
    release(mA)

    def hT_bufs(g):
        return B_hT[4 * g:4 * g + 4]

    def proj_feat(wt, bw, coff, g, bank):
        for c in range(8):
            K.op("pe", lambda e, c=c: e.matmul(ps[bank][:], lhsT=wt[:, c, coff:coff + 128],
                                                rhs=hT[:, c, g * 512:(g + 1) * 512], start=(c == 0), stop=(c == 7)),
                 r=list(bw) + hT_bufs(g), w=[B_ps[bank]])

    def proj_tok(wt, bw, coff, ncols, t, out_ap, bank):
        for c in range(8):
            K.op("pe", lambda e, c=c: e.matmul(out_ap, lhsT=hT[:, c, t * 128:(t + 1) * 128],
                                                rhs=wt[:, c, coff:coff + ncols], start=(c == 0), stop=(c == 7)),
                 r=list(bw) + [B_hT[t]], w=[B_ps[bank]])

    xs_tok = sb("xs_tok", [128, NT, 1024], BF16)
    B_xs = [Buf() for _ in range(NT)]
    mC = mark()
    Btok = sb("Btok", [128, NT, 256], BF16)
    B_Btok = [Buf() for _ in range(NT)]
    BT = sb("BT", [128, 2, L], BF16)
    B_BT = [Buf(), Buf()]
    CT = sb("CT", [128, 2, L], BF16)
    B_CT = [Buf(), Buf()]
    wz = sb("wz", [128, 8, 1040], BF16)
    B_wz = [Buf(), Buf(), Buf()]
    K.dma("pool", wz[:, :, 1024:1040], w_in_v[:, :, C_DT:C_DT + 16], w=[B_wz[2]])
    K.dma("pool", wz[:, :, 0:512], w_in_v[:, :, 0:512], w=[B_wz[0]])
    K.dma("pool", wz[:, :, 512:1024], w_in_v[:, :, 512:1024], w=[B_wz[1]])
    mC1 = mark()
    xraw = [sb("xraw%d" % i, [128, 3 + L]) for i in range(3)]
    B_xraw = [[Buf() for _ in range(5)] for _ in range(3)]
    acc = [sb("acc%d" % i, [128, L]) for i in range(2)]
    B_acc = [Buf(), Buf()]
    xcT = [sb("xcT%d" % i, [128, L], BF16) for i in range(2)]
    B_xcT = [Buf(), Buf()]
    for i in range(3):
        K.op("pool", lambda e, i=i: e.memset(xraw[i][:, 0:3], 0.0), w=[B_xraw[i][4]])
    evq = [0]

    def evac_eng():
        evq[0] += 1
        return "act" if evq[0] % 2 == 0 else "dve"

    def copy_op(eng, out_ap, in_ap, r, w):
        if eng == "act":
            K.op("act", lambda e: e.activation(out=out_ap, in_=in_ap, func=AF.Copy), r=r, w=w)
        else:
            K.op(eng, lambda e: e.tensor_copy(out=out_ap, in_=in_ap), r=r, w=w)

    c1w = {}

    def projC(cbk):
        if cbk == 0:
            c1w["wt"], c1w["bw"] = pre_w["c1"]
        elif cbk % 4 == 0:
            c1w["wt"], c1w["bw"] = load_w([(C_XBC + (cbk // 4) * 512, 512)])
        wt, bw = c1w["wt"], c1w["bw"]
        j = cbk % 4
        xi = cbk % 3
        xr = xraw[xi]
        for g in range(4):
            bank = 2 + (cbk * 4 + g) % 4
            proj_feat(wt, bw, j * 128, g, bank)
            copy_op("act" if g < 3 else "dve", xr[:, 3 + g * 512:3 + (g + 1) * 512], ps[bank][:], [B_ps[bank]], [B_xraw[xi][g]])

    def idC(cbk):
        xi = cbk % 3
        ai = cbk % 2
        xr = xraw[xi]
        a_ = acc[ai]
        K.op("act", lambda e: e.activation(
            out=a_[:], in_=xr[:, 3:3 + L], func=AF.Identity,
            scale=convw[:, cbk * 4 + 3:cbk * 4 + 4], bias=convb[:, cbk:cbk + 1]),
            r=B_xraw[xi] + [B_par], w=[B_acc[ai]])

    def convC(cbk):
        xi = cbk % 3
        ai = cbk % 2
        xr = xraw[xi]
        a_ = acc[ai]
        for tap in (2, 1, 0):
            K.op("dve", lambda e, tap=tap: e.scalar_tensor_tensor(
                out=a_[:], in0=xr[:, tap:tap + L], scalar=convw[:, cbk * 4 + tap:cbk * 4 + tap + 1],
                in1=a_[:], op0=ALU.mult, op1=ALU.add),
                r=B_xraw[xi] + [B_par, B_acc[ai]], w=[B_acc[ai]])
        if cbk + 1 < 12:
            idC(cbk + 1)
        if cbk + 2 < 12:
            projC(cbk + 2)
        if cbk < 8:
            dst, bd = xcT[ai][:], B_xcT[ai]
        elif cbk < 10:
            dst, bd = BT[:, cbk - 8, :], B_BT[cbk - 8]
        else:
            dst, bd = CT[:, cbk - 10, :], B_CT[cbk - 10]
        K.op("act", lambda e: e.activation(out=dst, in_=a_[:], func=AF.Silu), r=[B_acc[ai]], w=[bd])
        if cbk < 10:
            for half in range(2):
                pi = half
                for i in range(8):
                    t = half * 8 + i
                    K.op("pe", lambda e, pi=pi, i=i, t=t: e.transpose(
                        out=psbf(pi)[:, i * 128:(i + 1) * 128], in_=dst[:, t * 128:(t + 1) * 128],
                        identity=CB(CI_ID)), r=[bd, B_cb], w=[B_ps[pi]])
                if cbk < 8:
                    oap = xs_tok[:, half * 8:(half + 1) * 8, cbk * 128:(cbk + 1) * 128]
                    wb_ = B_xs[half * 8:(half + 1) * 8]
                else:
                    oap = Btok[:, half * 8:(half + 1) * 8, (cbk - 8) * 128:(cbk - 7) * 128]
                    wb_ = B_Btok[half * 8:(half + 1) * 8]
                copy_op("act", oap, psbf(pi).rearrange("p (t l) -> p t l", t=8), [B_ps[pi]], wb_)

    projC(0)
    projC(1)
    idC(0)
    for cbk in range(12):
        convC(cbk)

    K.fence()
    release(mC1)
    dts = sb("dts", [128, 7, NT, 16])
    B_dts = Buf()
    exa = sb("exa", [128, NT, 48])
    B_exa = Buf()
    aneg = sb("aneg", [128, 16])
    B_aneg = Buf()
    K.op("act", lambda e: e.activation(out=aneg[:], in_=hv[:, 16:32], func=AF.Exp), r=[B_par], w=[B_aneg])
    K.op("dve", lambda e: e.tensor_scalar(out=aneg[:], in0=aneg[:], scalar1=-1.0, scalar2=None, op0=ALU.mult),
         r=[B_aneg], w=[B_aneg])
    for t in range(NT):
        proj_tok(wz, [B_wz[2]], 1024, 16, t, ps[2][:, t * 16:(t + 1) * 16], 2)
    K.op("dve", lambda e: e.tensor_tensor(out=dts[:, 0, :, :], in0=ps[2][:, 0:NT * 16].rearrange("p (t h) -> p t h", t=NT),
                                          in1=hv[:, 0:16].unsqueeze(1).to_broadcast([128, NT, 16]), op=ALU.add),
         r=[B_ps[2], B_par], w=[B_dts])
    K.op("dve", lambda e: e.tensor_scalar(out=dts[:, 2, :, :], in0=dts[:, 0, :, :], scalar1=-1.0, scalar2=None, op0=ALU.mult),
         r=[B_dts], w=[B_dts])
    K.op("dve", lambda e: e.tensor_tensor(out=dts[:, 1, :, :], in0=dts[:, 0, :, :], in1=dts[:, 2, :, :], op=ALU.min),
         r=[B_dts], w=[B_dts])
    K.op("act", lambda e: e.activation(out=dts[:, 2, :, :], in_=dts[:, 1, :, :], func=AF.Exp, scale=1.0),
         r=[B_dts], w=[B_dts])
    K.op("act", lambda e: e.activation(out=dts[:, 3, :, :], in_=dts[:, 2, :, :], func=AF.Ln, bias=1.0, scale=1.0),
         r=[B_dts], w=[B_dts])
    K.op("dve", lambda e: e.scalar_tensor_tensor(out=dts[:, 4, :, :].rearrange("p t h -> p (t h)"),
                                                 in0=dts[:, 0, :, :].rearrange("p t h -> p (t h)"), scalar=0.0,
                                                 in1=dts[:, 3, :, :].rearrange("p t h -> p (t h)"),
                                                 op0=ALU.max, op1=ALU.add), r=[B_dts], w=[B_dts])
    K.op("dve", lambda e: e.tensor_tensor(out=dts[:, 5, :, :], in0=dts[:, 4, :, :],
                                          in1=aneg[:, :].unsqueeze(1).to_broadcast([128, NT, 16]), op=ALU.mult),
         r=[B_dts, B_aneg], w=[B_dts])
    for t in range(NT):
        bank = 3 + t // 8
        for k_, ci in enumerate((CI_LE, CI_GT, CI_ONES)):
            col = (t % 8) * 48 + k_ * 16
            K.op("pe", lambda e, bank=bank, col=col, ci=ci, t=t: e.matmul(
                ps[bank][:, col:col + 16], lhsT=CF(ci), rhs=dts[:, 5, t, :], start=True, stop=True),
                r=[B_cf, B_dts], w=[B_ps[bank]])
    for hb_ in range(2):
        K.op("act", lambda e, hb_=hb_: e.activation(
            out=exa[:, hb_ * 8:(hb_ + 1) * 8, :].rearrange("p t k -> p (t k)"), in_=ps[3 + hb_][:, 0:384], func=AF.Exp),
            r=[B_ps[3 + hb_]], w=[B_exa])
    K.op("dve", lambda e: e.tensor_tensor(out=dts[:, 6, :, :], in0=dts[:, 4, :, :], in1=exa[:, :, 16:32], op=ALU.mult),
         r=[B_dts, B_exa], w=[B_dts])

    sz = [sb("sz%d" % i, [128, 1024], BF16) for i in range(2)]
    B_sz = [[Buf(), Buf()], [Buf(), Buf()]]
    xw = [sb("xw%d" % i, [128, 1024], BF16) for i in range(2)]
    B_xw = [Buf(), Buf()]
    msc = sb("msc", [128, 256])
    B_msc = Buf()
    Lmat = sb("Lmat", [128, 16, 128])
    B_Lmat = Buf()
    dec = [sb("dec%d" % i, [128, 512]) for i in range(2)]
    B_dec = [Buf(), Buf()]
    attnT = sb("attnT", [128, 16, 128], BF16)
    B_attn = [Buf() for _ in range(16)]
    yA = [sb("yA%d" % i, [128, 1024]) for i in range(2)]
    B_yA = [[Buf(), Buf()], [Buf(), Buf()]]
    yB = sb("yB", [128, 1024])
    B_yB = [Buf(), Buf()]
    gn = sb("gn", [128, 1024], BF16)
    B_gn = [Buf(), Buf()]
    prev = sb("prev", [128, 1024])
    B_prev = [Buf(), Buf()]
    prevbf = sb("prevbf", [128, 1024], BF16)
    B_prevbf = [Buf(), Buf()]
    sst = sb("sst", [128, 8])
    B_sst = [Buf(), Buf()]
    B_sst2 = Buf()

    def front(c):
        tok = slice(c * 128, (c + 1) * 128)
        xw_, bxw = xw[c % 2], B_xw[c % 2]
        yA_, byA = yA[c % 2], B_yA[c % 2]
        K.op("dve", lambda e: e.tensor_tensor(
            out=Lmat[:], in0=CF(CI_GT).unsqueeze(1).to_broadcast([128, 16, 128]),
            in1=dts[:, 5, c, :].unsqueeze(2).to_broadcast([128, 16, 128]), op=ALU.mult),
            r=[B_cf, B_dts], w=[B_Lmat])
        yield
        for g in range(2):
            K.op("pe", lambda e, g=g: e.matmul(ps[2][:, g * 128:(g + 1) * 128], lhsT=BT[:, g, tok], rhs=CT[:, g, tok],
                                               start=True, stop=True),
                 r=[B_BT[g], B_CT[g]], w=[B_ps[2]])
        K.op("dve", lambda e: e.tensor_tensor(out=msc[:].rearrange("p (g l) -> p g l", g=2),
                                              in0=ps[2][:, 0:256].rearrange("p (g l) -> p g l", g=2),
                                              in1=CF(CI_LE).unsqueeze(1).to_broadcast([128, 2, 128]), op=ALU.mult),
             r=[B_ps[2], B_cf], w=[B_msc])
        yield

        def seg(r_):
            bank = 3 + r_ % 2
            for hh in range(4):
                h = r_ * 4 + hh
                K.op("pe", lambda e, hh=hh, h=h: e.matmul(
                    ps[bank][:, hh * 128:(hh + 1) * 128], lhsT=Lmat[:, h, :], rhs=CF(CI_LE), start=True, stop=True),
                    r=[B_Lmat, B_cf], w=[B_ps[bank]])
            d_, bd_ = dec[r_ % 2], B_dec[r_ % 2]
            K.op("act", lambda e: e.activation(out=d_[:], in_=ps[bank][:], func=AF.Exp), r=[B_ps[bank]], w=[bd_])

        seg(0)
        yield
        K.op("dve", lambda e: e.tensor_tensor(
            out=xw_[:].rearrange("p (h d) -> p h d", h=16), in0=xs_tok[:, c, :].rearrange("p (h d) -> p h d", h=16),
            in1=dts[:, 6, c, :].unsqueeze(2).to_broadcast([128, 16, 64]), op=ALU.mult),
            r=[B_xs[c], B_dts], w=[bxw])
        yield
        K.op("dve", lambda e: e.tensor_tensor(
            out=yA_[:].rearrange("p (h d) -> p h d", h=16), in0=xs_tok[:, c, :].rearrange("p (h d) -> p h d", h=16),
            in1=hv[:, 32:48].unsqueeze(2).to_broadcast([128, 16, 64]), op=ALU.mult),
            r=[B_xs[c], B_par], w=byA)
        yield
        for r_ in range(4):
            if r_ + 1 < 4:
                seg(r_ + 1)
                yield
            d_, bd_ = dec[r_ % 2], B_dec[r_ % 2]
            for hh in range(4):
                h = r_ * 4 + hh
                g = h // 8
                K.op("dve", lambda e, hh=hh, h=h, g=g, d_=d_: e.scalar_tensor_tensor(
                    out=attnT[:, h, :], in0=d_[:, hh * 128:(hh + 1) * 128], scalar=dts[:, 4, c, h:h + 1],
                    in1=msc[:, g * 128:(g + 1) * 128], op0=ALU.mult, op1=ALU.mult),
                    r=[bd_, B_dts, B_msc], w=[B_attn[h]])
                if hh % 2 == 1:
                    yield
            for hh in range(4):
                h = r_ * 4 + hh
                g = h // 8
                K.op("pe", lambda e, h=h, g=g: e.matmul(
                    ps[5 + g][:, (h % 8) * 64:(h % 8 + 1) * 64], lhsT=attnT[:, h, :], rhs=xs_tok[:, c, h * 64:(h + 1) * 64],
                    start=True, stop=True), r=[B_attn[h], B_xs[c]], w=[B_ps[5 + g]])
            yield
        for g in range(2):
            K.op("dve", lambda e, g=g: e.tensor_tensor(out=yA_[:, g * 512:(g + 1) * 512], in0=ps[5 + g][:],
                                                       in1=yA_[:, g * 512:(g + 1) * 512], op=ALU.add),
                 r=[B_ps[5 + g], byA[g]], w=[byA[g]])
            yield

    def back(c):
        tok = slice(c * 128, (c + 1) * 128)
        s_, bs_ = sz[c % 2], B_sz[c % 2]
        xw_, bxw = xw[c % 2], B_xw[c % 2]
        yA_, byA = yA[c % 2], B_yA[c % 2]
        if c > 0:
            for g in range(2):
                K.op("pe", lambda e, g=g: e.matmul(ps[7][:], lhsT=CT[:, g, tok], rhs=prevbf[:, g * 512:(g + 1) * 512],
                                                   start=True, stop=True),
                     r=[B_CT[g], B_prevbf[g]], w=[B_ps[7]])
                K.op("dve", lambda e, g=g: e.tensor_tensor(
                    out=yB[:, g * 512:(g + 1) * 512].rearrange("p (h d) -> p h d", h=8),
                    in0=ps[7][:].rearrange("p (h d) -> p h d", h=8),
                    in1=exa[:, c, g * 8:(g + 1) * 8].unsqueeze(2).to_broadcast([128, 8, 64]), op=ALU.mult),
                    r=[B_ps[7], B_exa], w=[B_yB[g]])
                yield
        if c < NT - 1:
            for g in range(2):
                bank = 1 if g == 0 else 7
                K.op("pe", lambda e, g=g, bank=bank: e.matmul(
                    ps[bank][:], lhsT=Btok[:, c, g * 128:(g + 1) * 128], rhs=xw_[:, g * 512:(g + 1) * 512], start=True, stop=True),
                    r=[B_Btok[c], bxw], w=[B_ps[bank]])
                pv = prev[:, g * 512:(g + 1) * 512]
                if c == 0:
                    K.op("dve", lambda e, pv=pv, bank=bank: e.tensor_copy(out=pv, in_=ps[bank][:]), r=[B_ps[bank]], w=[B_prev[g]])
                else:
                    K.op("dve", lambda e, pv=pv, g=g: e.tensor_tensor(
                        out=pv.rearrange("p (h d) -> p h d", h=8), in0=pv.rearrange("p (h d) -> p h d", h=8),
                        in1=exa[:, c, 32 + g * 8:32 + (g + 1) * 8].unsqueeze(2).to_broadcast([128, 8, 64]), op=ALU.mult),
                        r=[B_prev[g], B_exa], w=[B_prev[g]])
                    K.op("dve", lambda e, pv=pv, bank=bank: e.tensor_tensor(out=pv, in0=ps[bank][:], in1=pv, op=ALU.add),
                         r=[B_ps[bank], B_prev[g]], w=[B_prev[g]])
                K.op("act", lambda e, pv=pv, g=g: e.activation(out=prevbf[:, g * 512:(g + 1) * 512], in_=pv, func=AF.Copy),
                     r=[B_prev[g]], w=[B_prevbf[g]])
                yield
        for hf in range(2):
            proj_tok(wz, [B_wz[hf]], hf * 512, 512, c, ps[hf][:], hf)
            K.op("act", lambda e, hf=hf: e.activation(out=s_[:, hf * 512:(hf + 1) * 512], in_=ps[hf][:], func=AF.Silu),
                 r=[B_ps[hf]], w=[bs_[hf]])
        yield
        if c > 0:
            for g in range(2):
                hs = slice(g * 512, (g + 1) * 512)
                K.op("dve", lambda e, hs=hs: e.tensor_tensor(out=yA_[:, hs], in0=yA_[:, hs], in1=yB[:, hs], op=ALU.add),
                     r=[byA[g], B_yB[g]], w=[byA[g]])
                yield
        for g in range(2):
            hs = slice(g * 512, (g + 1) * 512)
            K.op("dve", lambda e, hs=hs: e.tensor_tensor(out=yA_[:, hs], in0=yA_[:, hs], in1=s_[:, hs], op=ALU.mult),
                 r=[byA[g], bs_[g]], w=[byA[g]])
            yield
        for g in range(2):
            K.op("act", lambda e, g=g: e.activation(out=yB[:, g * 512:(g + 1) * 512], in_=yA_[:, g * 512:(g + 1) * 512],
                                                    func=AF.Square, accum_out=sst[:, g:g + 1]),
                 r=[byA[g]], w=[B_yB[g], B_sst[g]])
        K.op("act", lambda e: e.activation(out=sst[:, 2:4], in_=sst[:, 0:2], func=AF.Ln, bias=EPS, scale=1.0 / 512),
             r=B_sst, w=[B_sst2])
        K.op("act", lambda e: e.activation(out=sst[:, 4:6], in_=sst[:, 2:4], func=AF.Exp, scale=-0.5), r=[B_sst2], w=[B_sst2])
        yield
        for g in range(2):
            K.op("dve", lambda e, g=g: e.scalar_tensor_tensor(
                out=gn[:, g * 512:(g + 1) * 512], in0=yA_[:, g * 512:(g + 1) * 512], scalar=sst[:, 4 + g:5 + g],
                in1=ssdn[:, g * 512:(g + 1) * 512], op0=ALU.mult, op1=ALU.mult),
                r=[byA[g], B_sst2, B_par], w=[B_gn[g]])
            yield
        for i in range(8):
            K.op("pe", lambda e, i=i: e.transpose(out=psbf(0)[:, i * 128:(i + 1) * 128], in_=gn[:, i * 128:(i + 1) * 128],
                                                  identity=CB(CI_ID)), r=[B_gn[i // 4], B_cb], w=[B_ps[0]])
        K.op("act", lambda e: e.activation(out=xs_tok[:, c, :], in_=psbf(0), func=AF.Copy), r=[B_ps[0]], w=[B_xs[c]])
        yield

    def interleave(gens, weights):
        active = [[g, w] for g, w in zip(gens, weights)]
        while active:
            for it in list(active):
                for _ in range(it[1]):
                    try:
                        next(it[0])
                    except StopIteration:
                        active.remove(it)
                        break

    interleave([front(0)], [1])
    for c in range(NT):
        gens = [back(c)]
        wts = [1]
        if c + 1 < NT:
            gens.append(front(c + 1))
            wts.append(2)
        interleave(gens, wts)

    ynT = xs_tok
    pre_w["ip0a"] = load_w([(C_Q, 256), (C_K, 256)])
    pre_w["ip0b"] = load_w([(C_V, 256)])
    K.fence()
    release(mC)
    if upto == "C":
        return finish([dump_bf("ynT", ynT[:], [128, NT, 1024], B_xs), ("dt", dts[:, 4, :, :].rearrange("p t h -> p (t h)"), [B_dts])])

    ogT = sb("ogT", [128, 8, L], BF16)
    B_og = [[[Buf() for _ in range(4)] for _ in range(2)] for _ in range(8)]
    mB = mark()
    qz = [sb("qz%d" % i, [128, 4, L], BF16) for i in range(2)]
    B_qz = [[[Buf() for _ in range(4)] for _ in range(4)] for _ in range(2)]
    B_qzero = Buf()
    kT = [sb("kT%d" % i, [128, 2, L], BF16) for i in range(2)]
    B_k = [[[Buf() for _ in range(4)] for _ in range(2)] for _ in range(2)]
    vtok = [sb("vtok%d" % i, [128, NT, 256], BF16) for i in range(2)]
    B_v = [[Buf() for _ in range(NT)] for _ in range(2)]
    et = [sb("et%d" % i, [128, 512]) for i in range(2)]
    B_et = [Buf(), Buf()]
    spb = [sb("spb%d" % i, [128, 512], BF16) for i in range(3)]
    B_sp = [Buf(), Buf(), Buf()]
    Ssum = [sb("Ssum%d" % i, [128, 512], BF16) for i in range(4)]
    B_S = [Buf() for _ in range(4)]
    wtb = [sb("wtb%d" % i, [128, 512], BF16) for i in range(3)]
    B_wt = [Buf(), Buf(), Buf()]
    uq = [0]
    gq = [0]
    NAB = 5
    for par in range(2):
        for hl4 in range(4):
            zr = slice(64, 128) if hl4 % 2 == 0 else slice(0, 64)
            K.op("pool", lambda e, par=par, hl4=hl4, zr=zr: e.memset(qz[par][zr, hl4, :], 0.0), w=[B_qzero])

    def inproj(qt):
        par = qt % 2
        if qt == 0:
            wt, bw = pre_w["ip0a"]
            wt2, bw2 = pre_w["ip0b"]
        else:
            wt, bw = load_w([(C_Q + qt * 256, 256), (C_K + qt * 256, 256)])
            wt2, bw2 = load_w([(C_V + qt * 256, 256)])
        yield
        rot = [0]

        def nb_():
            if qt != 0:
                return 5
            rot[0] += 1
            return 5 + rot[0] % 3
        for j in range(2):
            for g in range(4):
                bk = nb_()
                proj_feat(wt, bw, j * 128, g, bk)
                for a_ in range(2):
                    pr_ = slice(a_ * 64, a_ * 64 + 64)
                    hl_ = 2 * j + a_
                    K.op("dve", lambda e, hl_=hl_, pr_=pr_, g=g, bk=bk: e.tensor_scalar(
                        out=qz[par][pr_, hl_, g * 512:(g + 1) * 512], in0=ps[bk][pr_, :], scalar1=0.125, scalar2=None,
                        op0=ALU.mult), r=[B_ps[bk]], w=[B_qz[par][hl_][g]])
                yield
        for j in range(2):
            for g in range(4):
                bk = nb_()
                proj_feat(wt, bw, 256 + j * 128, g, bk)
                K.op("dve", lambda e, j=j, g=g, bk=bk: e.tensor_copy(out=kT[par][:, j, g * 512:(g + 1) * 512], in_=ps[bk][:]),
                     r=[B_ps[bk]], w=[B_k[par][j][g]])
                yield
        for t in range(NT):
            bk = nb_()
            proj_tok(wt2, bw2, 0, 256, t, ps[bk][:, 0:256], bk)
            K.op("dve", lambda e, t=t, bk=bk: e.tensor_copy(out=vtok[par][:, t, :], in_=ps[bk][:, 0:256]),
                 r=[B_ps[bk]], w=[B_v[par][t]])
            yield

    def attention(qt):
        par = qt % 2
        tasks = []
        for hl in range(4):
            for Q in range(4):
                u = uq[0]
                uq[0] += 1
                nblk = 4 * Q + 4
                q0p = None
                for idx in range(nblk):
                    c = nblk - 1 - idx
                    i = c - 4 * Q
                    q0 = max(i, 0) * 128
                    tasks.append(dict(hl=hl, Q=Q, u=u, idx=idx, c=c, i=i, q0=q0, q0p=q0p, gi=gq[0]))
                    gq[0] += 1
                    q0p = q0

        def stageA(T):
            gi, q0, q0p, idx, c, hl, Q = T["gi"], T["q0"], T["q0p"], T["idx"], T["c"], T["hl"], T["Q"]
            bA = gi % NAB
            j = hl // 2
            qs = Q * 512 + q0
            nq = 512 - q0
            K.op("pe", lambda e: e.matmul(ps[bA][:, q0:512], lhsT=kT[par][:, j, c * 128:(c + 1) * 128],
                                          rhs=qz[par][:, hl, qs:qs + nq], start=True, stop=False, skip_group_check=True),
                 r=[B_qz[par][hl][Q], B_qzero, B_k[par][j][c // 4]], w=[B_ps[bA]])
            if T["i"] >= 0:
                K.op("pe", lambda e: e.matmul(ps[bA][:, q0:q0 + 128], lhsT=CB(CI_ID), rhs=CB(CI_MASKM),
                                              start=False, stop=True, skip_group_check=True), r=[B_cb], w=[B_ps[bA]])
            e_, be = et[gi % 2], B_et[gi % 2]
            K.op("act", lambda e: e.activation(out=e_[:, q0:512], in_=ps[bA][:, q0:512], func=AF.Exp), r=[B_ps[bA]], w=[be])

        def stageA2(T):
            gi, q0, q0p, idx, c = T["gi"], T["q0"], T["q0p"], T["idx"], T["c"]
            e_, be = et[gi % 2], B_et[gi % 2]
            s__, bsp = spb[gi % 3], B_sp[gi % 3]
            K.op("act", lambda e: e.activation(out=s__[:, q0:512], in_=e_[:, q0:512], func=AF.Ln, bias=1.0, scale=1.0),
                 r=[be], w=[bsp])
            if c > 0:
                So, bSo = Ssum[gi % 4], B_S[gi % 4]
                Sn, bSn = Ssum[(gi + 1) % 4], B_S[(gi + 1) % 4]
                if idx == 0:
                    K.op("dve", lambda e: e.tensor_copy(out=Sn[:, q0:512], in_=s__[:, q0:512]), r=[bsp], w=[bSn])
                else:
                    K.op("dve", lambda e: e.tensor_tensor(out=Sn[:, q0p:512], in0=So[:, q0p:512], in1=s__[:, q0p:512], op=ALU.add),
                         r=[bSo, bsp], w=[bSn])
                    if q0 < q0p:
                        K.op("dve", lambda e: e.tensor_copy(out=Sn[:, q0:q0p], in_=s__[:, q0:q0p]), r=[bsp], w=[bSn])

        def stageB(T):
            gi, q0, q0p, idx = T["gi"], T["q0"], T["q0p"], T["idx"]
            bA = gi % NAB
            s__, bsp = spb[gi % 3], B_sp[gi % 3]
            K.op("pe", lambda e: e.matmul(ps[bA][:, q0:512], lhsT=CB(CI_NEGGE), rhs=s__[:, q0:512], start=False, stop=(idx == 0),
                                          skip_group_check=True), r=[bsp, B_cb], w=[B_ps[bA]])
            if idx > 0:
                So, bSo = Ssum[gi % 4], B_S[gi % 4]
                K.op("pe", lambda e: e.matmul(ps[bA][:, q0p:512], lhsT=CB(CI_NEGONES), rhs=So[:, q0p:512], start=False, stop=True,
                                              skip_group_check=True), r=[bSo, B_cb], w=[B_ps[bA]])
            w_, bw_ = wtb[gi % 3], B_wt[gi % 3]
            K.op("act", lambda e: e.activation(out=w_[:, q0:512], in_=ps[bA][:, q0:512], func=AF.Exp), r=[B_ps[bA]], w=[bw_])

        def stageC(T):
            gi, q0, idx, c, hl, Q, u = T["gi"], T["q0"], T["idx"], T["c"], T["hl"], T["Q"], T["u"]
            j = hl // 2
            hp = qt * 2 + j
            po = (hl % 2) * 64
            pr = slice(po, po + 64)
            bO = 6 + u % 2
            w_, bw_ = wtb[gi % 3], B_wt[gi % 3]
            K.op("pe", lambda e: e.matmul(ps[bO][:, q0:512], lhsT=vtok[par][:, c, j * 128:(j + 1) * 128], rhs=w_[:, q0:512],
                                          start=(idx == 0), stop=(c == 0), skip_group_check=True),
                 r=[B_v[par][c], bw_], w=[B_ps[bO]])
            if c == 0:
                K.op("dve", lambda e: e.tensor_copy(out=ogT[pr, hp, Q * 512:(Q + 1) * 512], in_=ps[bO][pr, :]),
                     r=[B_ps[bO]], w=[B_og[hp][hl % 2][Q]])

        NTK = len(tasks)
        LB, LC = 2, 3
        for s in range(NTK + LC):
            if s < NTK:
                stageA(tasks[s])
            if 0 <= s - LB < NTK:
                stageB(tasks[s - LB])
            if s < NTK:
                stageA2(tasks[s])
            if 0 <= s - LC < NTK:
                stageC(tasks[s - LC])
            yield

    def drain_(gen):
        if gen is not None:
            for _ in gen:
                pass

    drain_(inproj(0))
    for qt in range(4):
        bg = inproj(qt + 1) if qt + 1 < 4 else None
        n_ = 0
        for _ in attention(qt):
            n_ += 1
            if bg is not None and n_ % 3 == 0:
                try:
                    next(bg)
                except StopIteration:
                    bg = None
        drain_(bg)

    B_og_all = [B_og[hp][a][Q] for hp in range(8) for a in range(2) for Q in range(4)]
    pre_w["zsb0"] = load_w([(C_ZSB, 512)])
    pre_w["zsb1"] = load_w([(C_ZSB + 512, 512)])
    K.fence()
    release(mB)
    if upto == "B":
        return finish([dump_bf("ogT", ogT[:], [128, 8, L], B_og_all)])

    mergedT = sb("mergedT", [128, 8, L], BF16)
    B_mg = [Buf() for _ in range(NT)]
    wA = sb("wA", [128, 8, 1024], BF16)
    wB = sb("wB", [128, 8, 1024], BF16)
    B_wA = [Buf(), Buf()]
    B_wB = [Buf(), Buf()]
    wssd_v = wssd_d.rearrange("(c p) e -> p c e", p=128)
    wsb_v = wsb_d.rearrange("(c p) e -> p c e", p=128)
    wout_v = wout_d.rearrange("(c p) e -> p c e", p=128)
    for hf in range(2):
        K.dma("pool", wA[:, :, hf * 512:(hf + 1) * 512], wssd_v[:, :, hf * 512:(hf + 1) * 512], w=[B_wA[hf]])
        K.dma("pool", wB[:, :, hf * 512:(hf + 1) * 512], wsb_v[:, :, hf * 512:(hf + 1) * 512], w=[B_wB[hf]])
    mF1 = mark()
    szt = [sb("szt%d" % i, [128, 512], BF16) for i in range(2)]
    B_szt = [Buf(), Buf()]
    for i2 in range(2):
        wt, bw = pre_w["zsb%d" % i2]
        for j in range(4):
            hp = i2 * 4 + j
            for g in range(4):
                k_ = (j * 4 + g) % 2
                bank = 4 * k_
                proj_feat(wt, bw, j * 128, g, bank)
                K.op("act", lambda e, k_=k_, bank=bank: e.activation(out=szt[k_][:], in_=ps[bank][:], func=AF.Silu),
                     r=[B_ps[bank]], w=[B_szt[k_]])
                ogb_ = [B_og[hp][0][g], B_og[hp][1][g]]
                K.op("dve", lambda e, k_=k_, hp=hp, g=g: e.tensor_tensor(
                    out=ogT[:, hp, g * 512:(g + 1) * 512], in0=ogT[:, hp, g * 512:(g + 1) * 512], in1=szt[k_][:], op=ALU.mult),
                    r=ogb_ + [B_szt[k_]], w=ogb_)
    gs = [sb("gs%d" % i, [128, 512]) for i in range(2)]
    B_gs = [Buf(), Buf()]
    tt = [sb("tt%d" % i, [128, 512]) for i in range(2)]
    B_tt = [Buf(), Buf()]
    for d in range(8):
        wt, bw = load_w([(C_G + d * 128, 128), (C_G + 1024 + d * 128, 128)])
        if d == 5:
            K.dma("pool", wA[:, :, 0:512], wout_v[:, :, 0:512], w=[B_wA[0]])
        for g in range(4):
            st = (d * 4 + g) % 2
            ba, bb_, bg1, bg2 = 4 * st, 4 * st + 1, 4 * st + 2, 4 * st + 3
            yb = B_xs[4 * g:4 * g + 4]
            for c in range(8):
                K.op("pe", lambda e, c=c, ba=ba, d=d, g=g: e.matmul(
                    ps[ba][:], lhsT=wA[:, c, d * 128:(d + 1) * 128], rhs=ynT[:, 4 * g:4 * g + 4, c * 128:(c + 1) * 128],
                    start=(c == 0), stop=(c == 7)), r=[B_wA[d // 4]] + yb, w=[B_ps[ba]])
            ogb = [B_og[hp][a][g] for hp in range(8) for a in range(2)]
            for c in range(8):
                K.op("pe", lambda e, c=c, bb_=bb_, d=d, g=g: e.matmul(
                    ps[bb_][:], lhsT=wB[:, c, d * 128:(d + 1) * 128], rhs=ogT[:, c, g * 512:(g + 1) * 512],
                    start=(c == 0), stop=(c == 7)), r=[B_wB[d // 4]] + ogb, w=[B_ps[bb_]])
            proj_feat(wt, bw, 0, g, bg1)
            proj_feat(wt, bw, 128, g, bg2)
            for k_, (bg, bo) in enumerate(((bg1, ba), (bg2, bb_))):
                K.op("act", lambda e, k_=k_, bg=bg, d=d: e.activation(
                    out=gs[k_][:], in_=ps[bg][:], func=AF.Sigmoid, bias=bgate[:, k_ * 8 + d:k_ * 8 + d + 1], scale=1.0),
                    r=[B_ps[bg], B_par], w=[B_gs[k_]])
                K.op("dve", lambda e, k_=k_, bo=bo: e.tensor_tensor(out=tt[k_][:], in0=ps[bo][:], in1=gs[k_][:], op=ALU.mult),
                     r=[B_ps[bo], B_gs[k_]], w=[B_tt[k_]])
            K.op("dve", lambda e, d=d, g=g: e.tensor_tensor(out=mergedT[:, d, g * 512:(g + 1) * 512], in0=tt[0][:], in1=tt[1][:],
                                                            op=ALU.add), r=[B_tt[0], B_tt[1]], w=B_mg[4 * g:4 * g + 4])
    K.dma("pool", wA[:, :, 512:1024], wout_v[:, :, 512:1024], w=[B_wA[1]])
    K.fence()
    release(mF1)
    xb2 = [sb("xb2_%d" % i, [128, D]) for i in range(3)]
    B_xb2 = [Buf(), Buf(), Buf()]
    fin = [sb("fin0", [128, D])]
    B_fin = [Buf()]
    junk2 = sb("junk2", [128, 512], BF16)
    B_junk2 = Buf()
    st2 = sb("st2", [128, 4 * NT])
    B_st2 = [Buf() for _ in range(NT)]
    for t0_ in range(2):
        K.dma("sp", xb2[t0_][:], x_d[t0_ * 128:(t0_ + 1) * 128, :], w=[B_xb2[t0_]])
    def fin_mm(t):
        b0 = 2 * (t % 3)
        for hf in range(2):
            for c in range(8):
                K.op("pe", lambda e, c=c, hf=hf, bank=b0 + hf: e.matmul(
                    ps[bank][:], lhsT=mergedT[:, c, t * 128:(t + 1) * 128], rhs=wA[:, c, hf * 512:(hf + 1) * 512],
                    start=(c == 0), stop=(c == 7)), r=[B_wA[hf], B_mg[t]], w=[B_ps[b0 + hf]])

    def fin_stats(t):
        b0 = 2 * (t % 3)
        for hf in range(2):
            K.op("act", lambda e, hf=hf, bank=b0 + hf: e.activation(out=junk2[:], in_=ps[bank][:],
                                                                    func=AF.Square, accum_out=st2[:, 4 * t + hf:4 * t + hf + 1]),
                 r=[B_ps[b0 + hf]], w=[B_junk2, B_st2[t]])
        K.op("dve", lambda e: e.tensor_tensor(out=st2[:, 4 * t + 2:4 * t + 3], in0=st2[:, 4 * t:4 * t + 1],
                                              in1=st2[:, 4 * t + 1:4 * t + 2], op=ALU.add), r=[B_st2[t]], w=[B_st2[t]])
        K.op("act", lambda e: e.activation(out=st2[:, 4 * t + 3:4 * t + 4], in_=st2[:, 4 * t + 2:4 * t + 3], func=AF.Ln,
                                           bias=EPS, scale=1.0 / D), r=[B_st2[t]], w=[B_st2[t]])
        K.op("act", lambda e: e.activation(out=st2[:, 4 * t + 2:4 * t + 3], in_=st2[:, 4 * t + 3:4 * t + 4], func=AF.Exp,
                                           scale=-0.5), r=[B_st2[t]], w=[B_st2[t]])

    def fin_apply(t):
        xt, bx = xb2[t % 3], B_xb2[t % 3]
        if t + 2 < NT:
            K.dma("sp", xb2[(t + 2) % 3][:], x_d[(t + 2) * 128:(t + 3) * 128, :], w=[B_xb2[(t + 2) % 3]])
        f_, bf_ = fin[0], B_fin[0]
        b0 = 2 * (t % 3)
        for hf in range(2):
            K.op("dve", lambda e, hf=hf, bank=b0 + hf: e.scalar_tensor_tensor(
                out=f_[:, hf * 512:(hf + 1) * 512], in0=ps[bank][:], scalar=st2[:, 4 * t + 2:4 * t + 3],
                in1=npost[:, hf * 512:(hf + 1) * 512],
                op0=ALU.mult, op1=ALU.mult), r=[B_ps[b0 + hf], B_st2[t], B_par], w=[bf_])
        K.op("dve", lambda e: e.tensor_tensor(out=xt[:], in0=f_[:], in1=xt[:], op=ALU.add), r=[bf_, bx], w=[bx])
        out_ops.append(K.dma("sp", y_d[t * 128:(t + 1) * 128, :], xt[:], r=[bx]))

    fin_mm(0)
    fin_mm(1)
    fin_stats(0)
    for t in range(NT):
        if t + 2 < NT:
            fin_mm(t + 2)
        if t + 1 < NT:
            fin_stats(t + 1)
        fin_apply(t)
    K.wait_ops("sp", out_ops)
    K.emit()
    return nc


def prep_inputs(inputs):
    f = lambda a: np.ascontiguousarray(np.asarray(a, dtype=np.float32))
    x = f(inputs["x"])
    shared = {
        "w_in": f(inputs["w_in"][0]),
        "w_ssd_proj": f(inputs["w_ssd_proj"][0]),
        "w_sb_proj": f(inputs["w_sb_proj"][0]),
        "w_out": f(inputs["w_out"][0]),
        "cst": make_consts(),
        "npre_pc": f(np.asarray(inputs["norm_pre"][0]).reshape(8, 128).T),
        "npost_bc": f(np.broadcast_to(np.asarray(inputs["norm_post"][0])[None, :], (128, D))),
        "ssdn_bc": f(np.broadcast_to(np.asarray(inputs["ssd_norm"][0])[None, :], (128, D))),
        "bgate_pc": f(np.asarray(inputs["b_gate"][0]).reshape(16, 128).T),
        "convw_pc": f(np.asarray(inputs["conv_w"][0]).reshape(4, 12, 128).transpose(2, 1, 0).reshape(128, 48)),
        "convb_pc": f(np.asarray(inputs["conv_b"][0]).reshape(12, 128).T),
        "headvec_bc": f(np.broadcast_to(np.concatenate([np.asarray(inputs["dt_bias"][0]),
                                                         np.asarray(inputs["a_log"][0]),
                                                         np.asarray(inputs["d_skip"][0])])[None, :], (128, 48))),
    }
    in_maps = []
    for b in range(8):
        m = dict(shared)
        m["x"] = np.ascontiguousarray(x[b])
        in_maps.append(m)
    return in_maps


def kernel(**inputs):
    in_maps = prep_inputs(inputs)
    nc = build()
    res = run_bass_kernel_spmd(nc, in_maps, core_ids=list(range(8)))
    return np.stack([np.asarray(r["y"], dtype=np.float32) for r in res.results], axis=0)
```

```python
import numpy as np
import concourse.bass as bass
import concourse.mybir as mybir
from concourse.bass_utils import run_bass_kernel_spmd

F32 = mybir.dt.float32
BF16 = mybir.dt.bfloat16
AF = mybir.ActivationFunctionType
ALU = mybir.AluOpType

L = 2048
D = 1024
NT = 16
EPS = 1e-6
NDMA = 24

C_ZSSD, C_XBC, C_DT, C_Q, C_K, C_V, C_ZSB, C_G = 0, 1024, 2560, 2576, 3600, 4624, 5648, 6672


class Op:
    __slots__ = ("eng", "idx", "signal", "semval")

    def __init__(self, eng, idx):
        self.eng = eng
        self.idx = idx
        self.signal = False
        self.semval = None


class Buf:
    __slots__ = ("wr", "rd", "excl")

    def __init__(self, excl=False):
        self.wr = None
        self.rd = {}
        self.excl = excl


class Sched:
    ENG = ["pe", "act", "dve", "pool", "sp"]

    def __init__(self, nc):
        self.nc = nc
        self.prog = {n: [] for n in self.ENG}
        self.cnt = {n: 0 for n in self.ENG}
        self.seen = {n: {} for n in self.ENG}
        self.sem = {n: nc.alloc_semaphore("s_" + n) for n in ["pe", "act", "dve", "pool"]}
        self.dsem = [nc.alloc_semaphore("s_dma%d" % i) for i in range(NDMA)]
        self.dcnt = [0] * NDMA
        self.dlast = [None] * NDMA
        self.drr = 0
        self.drr_sw = 0

    def _deps(self, eng, reads, writes, extra=()):
        deps = {}

        def add(o):
            if o is None:
                return
            if o.eng == eng and eng == "pe":
                return
            cur = deps.get(o.eng)
            if cur is None or o.idx > cur.idx:
                deps[o.eng] = o

        for b in reads:
            add(b.wr)
            if b.excl:
                for o in b.rd.values():
                    if o.eng != eng:
                        add(o)
        for b in writes:
            add(b.wr)
            for o in b.rd.values():
                add(o)
        for o in extra:
            add(o)
        out = []
        for key, o in deps.items():
            if self.seen[eng].get(key, -1) >= o.idx:
                continue
            self.seen[eng][key] = o.idx
            o.signal = True
            out.append(o)
        return out

    def op(self, eng, fn, r=(), w=()):
        waits = self._deps(eng, r, w)
        o = Op(eng, self.cnt[eng])
        self.cnt[eng] += 1
        self.prog[eng].append(("op", waits, fn, o))
        for b in r:
            b.rd[eng] = o
        for b in w:
            b.wr = o
            b.rd = {}
        return o

    def dma(self, q, out, in_, r=(), w=()):
        half = NDMA // 2
        if q == "pool":
            j = half + self.drr_sw % half
            self.drr_sw += 1
        else:
            j = self.drr % half
            self.drr += 1
        key = "dma%d" % j
        extra = [self.dlast[j]] if self.dlast[j] is not None else []
        waits = self._deps(q, r, w, extra)
        o = Op(key, self.dcnt[j])
        self.dcnt[j] += 1
        o.signal = True
        o.semval = 16 * (o.idx + 1)
        self.dlast[j] = o
        self.prog[q].append(("dma", waits, (out, in_, j), o))
        for b in r:
            b.rd[key] = o
        for b in w:
            b.wr = o
            b.rd = {}
        return o

    def wait_ops(self, eng, ops):
        waits = self._deps(eng, (), (), ops)
        self.prog[eng].append(("wait", waits, None, None))

    def fence(self):
        last = []
        for n in ["pe", "act", "dve", "pool"]:
            for kind, waits, fn, o in reversed(self.prog[n]):
                if kind == "op":
                    last.append(o)
                    break
        last += [o for o in self.dlast if o is not None]
        for n in self.ENG:
            self.wait_ops(n, last)

    def _semh(self, o):
        if o.eng.startswith("dma"):
            return self.dsem[int(o.eng[3:])]
        return self.sem[o.eng]

    def emit(self):
        for n in ["pe", "act", "dve", "pool"]:
            c = 0
            for kind, waits, fn, o in self.prog[n]:
                if kind == "op" and o.signal:
                    c += 1
                    o.semval = c
        nc = self.nc

        def run(n, e):
            for kind, waits, fn, o in self.prog[n]:
                for d in waits:
                    e.wait_ge(self._semh(d), d.semval)
                if kind == "op":
                    ins = fn(e)
                    if o.signal:
                        ins.then_inc(self.sem[n], 1)
                elif kind == "dma":
                    out, in_, j = fn
                    e.dma_start(out=out, in_=in_).then_inc(self.dsem[j], 16)

        with nc.Block() as block:
            @block.tensor
            def _(e):
                run("pe", e)

            @block.scalar
            def _(e):
                run("act", e)

            @block.vector
            def _(e):
                run("dve", e)

            @block.gpsimd
            def _(e):
                run("pool", e)

            @block.sync
            def _(e):
                run("sp", e)


def make_consts():
    a = np.arange(128)
    ident = (a[:, None] == a[None, :]).astype(np.float32)
    le = (a[:, None] <= a[None, :]).astype(np.float32)
    gt = (a[:, None] > a[None, :]).astype(np.float32)
    ones = np.ones((128, 128), np.float32)
    negge = -(a[:, None] >= a[None, :]).astype(np.float32)
    negones = -ones
    maskm = np.where(a[:, None] >= a[None, :], -30000.0, 0.0).astype(np.float32)
    return np.concatenate([ident, le, gt, ones, negge, negones, maskm], axis=1)


NCONST = 7
(CI_ID, CI_LE, CI_GT, CI_ONES, CI_NEGGE, CI_NEGONES, CI_MASKM) = range(7)


def build(dbg=None, upto="all"):
    nc = bass.Bass("TRN2", target_bir_lowering=False)
    dbg = dbg or []

    def din(name, shape):
        return nc.dram_tensor(name, list(shape), F32, kind="ExternalInput").ap()

    x_d = din("x", [L, D])
    w_in = din("w_in", [D, 8720])
    wssd_d = din("w_ssd_proj", [D, D])
    wsb_d = din("w_sb_proj", [D, D])
    wout_d = din("w_out", [D, D])
    cst_d = din("cst", [128, NCONST * 128])
    npre_d = din("npre_pc", [128, 8])
    npost_d = din("npost_bc", [128, D])
    ssdn_d = din("ssdn_bc", [128, D])
    bgate_d = din("bgate_pc", [128, 16])
    convw_d = din("convw_pc", [128, 48])
    convb_d = din("convb_pc", [128, 12])
    hv_d = din("headvec_bc", [128, 48])
    y_d = nc.dram_tensor("y", [L, D], F32, kind="ExternalOutput").ap()
    dbg_d = {}
    for name, shape in dbg:
        dbg_d[name] = nc.dram_tensor("dbg_" + name, list(shape), F32, kind="ExternalOutput").ap()

    K = Sched(nc)

    def sb(name, shape, dt=F32):
        return nc.alloc_sbuf_tensor(name, list(shape), dt)

    cf = sb("cf", [128, NCONST * 128], F32)
    cb = sb("cb", [128, NCONST * 128], BF16)
    B_cf, B_cb = Buf(), Buf()

    def CF(i):
        return cf[:, i * 128:(i + 1) * 128]

    def CB(i):
        return cb[:, i * 128:(i + 1) * 128]

    npre = sb("npre", [128, 8])
    npost = sb("npost", [128, D])
    ssdn = sb("ssdn", [128, D])
    bgate = sb("bgate", [128, 16])
    convw = sb("convw", [128, 48])
    convb = sb("convb", [128, 12])
    hv = sb("hv", [128, 48])
    B_par = Buf()
    hT = sb("hT", [128, 8, L], BF16)
    B_hT = [Buf() for _ in range(NT)]
    ps = [nc.alloc_psum_tensor("ps%d" % i, [128, 512], F32) for i in range(8)]
    B_ps = [Buf(excl=True) for _ in range(8)]

    def psbf(i):
        return ps[i][:].bitcast(BF16)

    def load_consts():
        K.dma("pool", cb[:], cst_d, w=[B_cb])
        K.dma("sp", cf[:], cst_d, w=[B_cf])
        ops_ = []
        for t_, d_ in [(npre, npre_d), (npost, npost_d), (ssdn, ssdn_d), (bgate, bgate_d),
                       (convw, convw_d), (convb, convb_d), (hv, hv_d)]:
            ops_.append(K.dma("sp", t_[:], d_))
        B_par.wr = ops_[-1]
        return ops_

    out_ops = []

    wst = [sb("wst%d" % i, [128, 8, 512], BF16) for i in range(2)]
    B_wst = [[Buf() for _ in range(4)] for _ in range(2)]
    wrr = [0]
    w_in_v = w_in.rearrange("(c p) e -> p c e", p=128)

    def load_w(segs, src=None):
        i = wrr[0] % 2
        wrr[0] += 1
        srcv = w_in_v if src is None else src
        off = 0
        ops = []
        for si, (c0, n) in enumerate(segs):
            if si == 0:
                o = K.dma("pool", wst[i][:, :, off:off + n], srcv[:, :, c0:c0 + n], w=B_wst[i])
            else:
                o = K.dma("pool", wst[i][:, :, off:off + n], srcv[:, :, c0:c0 + n])
            ops.append(o)
            off += n
        for s in range(4):
            B_wst[i][s].wr = ops[s] if s < len(ops) else ops[0]
        return wst[i], B_wst[i]

    pre_w = {}

    mA = (nc.sbuf_base, nc.sbuf_top)
    xbuf = [sb("xbuf%d" % i, [128, D]) for i in range(4)]
    B_xbuf = [Buf(), Buf(), Buf(), Buf()]
    hb = [sb("hb%d" % i, [128, D], BF16) for i in range(2)]
    B_hb = [Buf(), Buf()]
    junk = sb("junk", [128, D], BF16)
    B_junk = Buf()
    junkA = sb("junkA", [128, D], BF16)
    B_junkA = Buf()
    stat = sb("stat", [128, 3 * NT])
    B_stat = [Buf() for _ in range(NT)]

    def a_stats(t):
        xt, bx = xbuf[t % 4], B_xbuf[t % 4]
        K.dma("sp", xt[:], x_d[t * 128:(t + 1) * 128, :], w=[bx])
        ss = stat[:, 3 * t:3 * t + 1]
        rms = stat[:, 3 * t + 1:3 * t + 2]
        rstd = stat[:, 3 * t + 2:3 * t + 3]
        if t % 2 == 0:
            K.op("dve", lambda e: e.scalar_tensor_tensor(out=junk[:], in0=xt[:], scalar=1.0, in1=xt[:], op0=ALU.mult, op1=ALU.mult,
                                                         accum_out=ss), r=[bx], w=[B_junk, B_stat[t]])
        else:
            K.op("act", lambda e: e.activation(out=junkA[:], in_=xt[:], func=AF.Square, accum_out=ss),
                 r=[bx], w=[B_junkA, B_stat[t]])
        K.op("act", lambda e: e.activation(out=rms, in_=ss, func=AF.Sqrt, bias=EPS, scale=1.0 / D),
             r=[B_stat[t]], w=[B_stat[t]])
        K.op("dve", lambda e: e.reciprocal(out=rstd, in_=rms), r=[B_stat[t]], w=[B_stat[t]])

    def a_apply(t):
        xt, bx = xbuf[t % 4], B_xbuf[t % 4]
        rstd = stat[:, 3 * t + 2:3 * t + 3]
        h_, bh = hb[t % 2], B_hb[t % 2]
        K.op("act", lambda e: e.activation(out=h_[:], in_=xt[:], func=AF.Copy, scale=rstd), r=[bx, B_stat[t]], w=[bh])
        pi = t % 2
        for c in range(8):
            K.op("pe", lambda e, c=c: e.transpose(out=psbf(pi)[:, c * 128:(c + 1) * 128],
                                                  in_=h_[:, c * 128:(c + 1) * 128], identity=CB(CI_ID)),
                 r=[bh, B_cb], w=[B_ps[pi]])

    def a_evac(t):
        pi = t % 2
        K.op("dve", lambda e: e.tensor_tensor(
            out=hT[:, :, t * 128:(t + 1) * 128],
            in0=psbf(pi).rearrange("p (c l) -> p c l", c=8),
            in1=npre[:, :].unsqueeze(2).to_broadcast([128, 8, 128]), op=ALU.mult),
            r=[B_ps[pi], B_par], w=[B_hT[t]])

    a_stats(0)
    a_stats(1)
    par_ops = load_consts()
    pre_w["c1"] = load_w([(C_XBC, 512)])
    K.wait_ops("dve", par_ops)
    K.wait_ops("act", par_ops)
    K.wait_ops("pe", par_ops)
    for t in range(-1, NT):
        if 0 <= t + 2 < NT:
            a_stats(t + 2)
        if 0 <= t + 1 < NT:
            a_apply(t + 1)
        if t >= 0:
            a_evac(t)

    def mark():
        return (nc.sbuf_base, nc.sbuf_top)

    def release(m):
        nc.sbuf_base, nc.sbuf_top = m

    def finish(dumps):
        for it in dumps:
            if it is None:
                continue
            name, ap, bufs = it
            out_ops.append(K.dma("sp", dbg_d[name], ap, r=bufs))
        K.wait_ops("sp", out_ops)
        K.emit()
        return nc

    def dump_bf(name, src_ap, shape, bufs):
        dst = dbg_d[name]
        if len(shape) == 3:
            dst = dst.rearrange("p (a b) -> p a b", a=shape[1])
        out_ops.append(K.dma("pool", dst, src_ap, r=bufs))
        return None

    if upto == "A":
        return finish([dump_bf("hT", hT[:], [128, 8, L], B_hT)])

    K.fence()
    release(mA)

    def hT_bufs(g):
        return B_hT[4 * g:4 * g + 4]

    def proj_feat(wt, bw, coff, g, bank):
        for c in range(8):
            K.op("pe", lambda e, c=c: e.matmul(ps[bank][:], lhsT=wt[:, c, coff:coff + 128],
                                                rhs=hT[:, c, g * 512:(g + 1) * 512], start=(c == 0), stop=(c == 7)),
                 r=list(bw) + hT_bufs(g), w=[B_ps[bank]])

    def proj_tok(wt, bw, coff, ncols, t, out_ap, bank):
        for c in range(8):
            K.op("pe", lambda e, c=c: e.matmul(out_ap, lhsT=hT[:, c, t * 128:(t + 1) * 128],
                                                rhs=wt[:, c, coff:coff + ncols], start=(c == 0), stop=(c == 7)),
                 r=list(bw) + [B_hT[t]], w=[B_ps[bank]])

    xs_tok = sb("xs_tok", [128, NT, 1024], BF16)
    B_xs = [Buf() for _ in range(NT)]
    mC = mark()
    Btok = sb("Btok", [128, NT, 256], BF16)
    B_Btok = [Buf() for _ in range(NT)]
    BT = sb("BT", [128, 2, L], BF16)
    B_BT = [Buf(), Buf()]
    CT = sb("CT", [128, 2, L], BF16)
    B_CT = [Buf(), Buf()]
    wz = sb("wz", [128, 8, 1040], BF16)
    B_wz = [Buf(), Buf(), Buf()]
    K.dma("pool", wz[:, :, 1024:1040], w_in_v[:, :, C_DT:C_DT + 16], w=[B_wz[2]])
    K.dma("pool", wz[:, :, 0:512], w_in_v[:, :, 0:512], w=[B_wz[0]])
    K.dma("pool", wz[:, :, 512:1024], w_in_v[:, :, 512:1024], w=[B_wz[1]])
    mC1 = mark()
    xraw = [sb("xraw%d" % i, [128, 3 + L]) for i in range(3)]
    B_xraw = [[Buf() for _ in range(5)] for _ in range(3)]
    acc = [sb("acc%d" % i, [128, L]) for i in range(2)]
    B_acc = [Buf(), Buf()]
    xcT = [sb("xcT%d" % i, [128, L], BF16) for i in range(2)]
    B_xcT = [Buf(), Buf()]
    for i in range(3):
        K.op("pool", lambda e, i=i: e.memset(xraw[i][:, 0:3], 0.0), w=[B_xraw[i][4]])
    evq = [0]

    def evac_eng():
        evq[0] += 1
        return "act" if evq[0] % 2 == 0 else "dve"

    def copy_op(eng, out_ap, in_ap, r, w):
        if eng == "act":
            K.op("act", lambda e: e.activation(out=out_ap, in_=in_ap, func=AF.Copy), r=r, w=w)
        else:
            K.op(eng, lambda e: e.tensor_copy(out=out_ap, in_=in_ap), r=r, w=w)

    c1w = {}

    def projC(cbk):
        if cbk == 0:
            c1w["wt"], c1w["bw"] = pre_w["c1"]
        elif cbk % 4 == 0:
            c1w["wt"], c1w["bw"] = load_w([(C_XBC + (cbk // 4) * 512, 512)])
        wt, bw = c1w["wt"], c1w["bw"]
        j = cbk % 4
        xi = cbk % 3
        xr = xraw[xi]
        for g in range(4):
            bank = 2 + (cbk * 4 + g) % 4
            proj_feat(wt, bw, j * 128, g, bank)
            copy_op("act" if g < 3 else "dve", xr[:, 3 + g * 512:3 + (g + 1) * 512], ps[bank][:], [B_ps[bank]], [B_xraw[xi][g]])

    def idC(cbk):
        xi = cbk % 3
        ai = cbk % 2
        xr = xraw[xi]
        a_ = acc[ai]
        K.op("act", lambda e: e.activation(
            out=a_[:], in_=xr[:, 3:3 + L], func=AF.Identity,
            scale=convw[:, cbk * 4 + 3:cbk * 4 + 4], bias=convb[:, cbk:cbk + 1]),
            r=B_xraw[xi] + [B_par], w=[B_acc[ai]])

    def convC(cbk):
        xi = cbk % 3
        ai = cbk % 2
        xr = xraw[xi]
        a_ = acc[ai]
        for tap in (2, 1, 0):
            K.op("dve", lambda e, tap=tap: e.scalar_tensor_tensor(
                out=a_[:], in0=xr[:, tap:tap + L], scalar=convw[:, cbk * 4 + tap:cbk * 4 + tap + 1],
                in1=a_[:], op0=ALU.mult, op1=ALU.add),
                r=B_xraw[xi] + [B_par, B_acc[ai]], w=[B_acc[ai]])
        if cbk + 1 < 12:
            idC(cbk + 1)
        if cbk + 2 < 12:
            projC(cbk + 2)
        if cbk < 8:
            dst, bd = xcT[ai][:], B_xcT[ai]
        elif cbk < 10:
            dst, bd = BT[:, cbk - 8, :], B_BT[cbk - 8]
        else:
            dst, bd = CT[:, cbk - 10, :], B_CT[cbk - 10]
        K.op("act", lambda e: e.activation(out=dst, in_=a_[:], func=AF.Silu), r=[B_acc[ai]], w=[bd])
        if cbk < 10:
            for half in range(2):
                pi = half
                for i in range(8):
                    t = half * 8 + i
                    K.op("pe", lambda e, pi=pi, i=i, t=t: e.transpose(
                        out=psbf(pi)[:, i * 128:(i + 1) * 128], in_=dst[:, t * 128:(t + 1) * 128],
                        identity=CB(CI_ID)), r=[bd, B_cb], w=[B_ps[pi]])
                if cbk < 8:
                    oap = xs_tok[:, half * 8:(half + 1) * 8, cbk * 128:(cbk + 1) * 128]
                    wb_ = B_xs[half * 8:(half + 1) * 8]
                else:
                    oap = Btok[:, half * 8:(half + 1) * 8, (cbk - 8) * 128:(cbk - 7) * 128]
                    wb_ = B_Btok[half * 8:(half + 1) * 8]
                copy_op("act", oap, psbf(pi).rearrange("p (t l) -> p t l", t=8), [B_ps[pi]], wb_)

    projC(0)
    projC(1)
    idC(0)
    for cbk in range(12):
        convC(cbk)

    K.fence()
    release(mC1)
    dts = sb("dts", [128, 7, NT, 16])
    B_dts = Buf()
    exa = sb("exa", [128, NT, 48])
    B_exa = Buf()
    aneg = sb("aneg", [128, 16])
    B_aneg = Buf()
    K.op("act", lambda e: e.activation(out=aneg[:], in_=hv[:, 16:32], func=AF.Exp), r=[B_par], w=[B_aneg])
    K.op("dve", lambda e: e.tensor_scalar(out=aneg[:], in0=aneg[:], scalar1=-1.0, scalar2=None, op0=ALU.mult),
         r=[B_aneg], w=[B_aneg])
    for t in range(NT):
        proj_tok(wz, [B_wz[2]], 1024, 16, t, ps[2][:, t * 16:(t + 1) * 16], 2)
    K.op("dve", lambda e: e.tensor_tensor(out=dts[:, 0, :, :], in0=ps[2][:, 0:NT * 16].rearrange("p (t h) -> p t h", t=NT),
                                          in1=hv[:, 0:16].unsqueeze(1).to_broadcast([128, NT, 16]), op=ALU.add),
         r=[B_ps[2], B_par], w=[B_dts])
    K.op("dve", lambda e: e.tensor_scalar(out=dts[:, 2, :, :], in0=dts[:, 0, :, :], scalar1=-1.0, scalar2=None, op0=ALU.mult),
         r=[B_dts], w=[B_dts])
    K.op("dve", lambda e: e.tensor_tensor(out=dts[:, 1, :, :], in0=dts[:, 0, :, :], in1=dts[:, 2, :, :], op=ALU.min),
         r=[B_dts], w=[B_dts])
    K.op("act", lambda e: e.activation(out=dts[:, 2, :, :], in_=dts[:, 1, :, :], func=AF.Exp, scale=1.0),
         r=[B_dts], w=[B_dts])
    K.op("act", lambda e: e.activation(out=dts[:, 3, :, :], in_=dts[:, 2, :, :], func=AF.Ln, bias=1.0, scale=1.0),
         r=[B_dts], w=[B_dts])
    K.op("dve", lambda e: e.scalar_tensor_tensor(out=dts[:, 4, :, :].rearrange("p t h -> p (t h)"),
                                                 in0=dts[:, 0, :, :].rearrange("p t h -> p (t h)"), scalar=0.0,
                                                 in1=dts[:, 3, :, :].rearrange("p t h -> p (t h)"),
                                                 op0=ALU.max, op1=ALU.add), r=[B_dts], w=[B_dts])
    K.op("dve", lambda e: e.tensor_tensor(out=dts[:, 5, :, :], in0=dts[:, 4, :, :],
                                          in1=aneg[:, :].unsqueeze(1).to_broadcast([128, NT, 16]), op=ALU.mult),
         r=[B_dts, B_aneg], w=[B_dts])
    for t in range(NT):
        bank = 3 + t // 8
        for k_, ci in enumerate((CI_LE, CI_GT, CI_ONES)):
            col = (t % 8) * 48 + k_ * 16
            K.op("pe", lambda e, bank=bank, col=col, ci=ci, t=t: e.matmul(
                ps[bank][:, col:col + 16], lhsT=CF(ci), rhs=dts[:, 5, t, :], start=True, stop=True),
                r=[B_cf, B_dts], w=[B_ps[bank]])
    for hb_ in range(2):
        K.op("act", lambda e, hb_=hb_: e.activation(
            out=exa[:, hb_ * 8:(hb_ + 1) * 8, :].rearrange("p t k -> p (t k)"), in_=ps[3 + hb_][:, 0:384], func=AF.Exp),
            r=[B_ps[3 + hb_]], w=[B_exa])
    K.op("dve", lambda e: e.tensor_tensor(out=dts[:, 6, :, :], in0=dts[:, 4, :, :], in1=exa[:, :, 16:32], op=ALU.mult),
         r=[B_dts, B_exa], w=[B_dts])

    sz = [sb("sz%d" % i, [128, 1024], BF16) for i in range(2)]
    B_sz = [[Buf(), Buf()], [Buf(), Buf()]]
    xw = [sb("xw%d" % i, [128, 1024], BF16) for i in range(2)]
    B_xw = [Buf(), Buf()]
    msc = sb("msc", [128, 256])
    B_msc = Buf()
    Lmat = sb("Lmat", [128, 16, 128])
    B_Lmat = Buf()
    dec = [sb("dec%d" % i, [128, 512]) for i in range(2)]
    B_dec = [Buf(), Buf()]
    attnT = sb("attnT", [128, 16, 128], BF16)
    B_attn = [Buf() for _ in range(16)]
    yA = [sb("yA%d" % i, [128, 1024]) for i in range(2)]
    B_yA = [[Buf(), Buf()], [Buf(), Buf()]]
    yB = sb("yB", [128, 1024])
    B_yB = [Buf(), Buf()]
    gn = sb("gn", [128, 1024], BF16)
    B_gn = [Buf(), Buf()]
    prev = sb("prev", [128, 1024])
    B_prev = [Buf(), Buf()]
    prevbf = sb("prevbf", [128, 1024], BF16)
    B_prevbf = [Buf(), Buf()]
    sst = sb("sst", [128, 8])
    B_sst = [Buf(), Buf()]
    B_sst2 = Buf()

    def front(c):
        tok = slice(c * 128, (c + 1) * 128)
        xw_, bxw = xw[c % 2], B_xw[c % 2]
        yA_, byA = yA[c % 2], B_yA[c % 2]
        K.op("dve", lambda e: e.tensor_tensor(
            out=Lmat[:], in0=CF(CI_GT).unsqueeze(1).to_broadcast([128, 16, 128]),
            in1=dts[:, 5, c, :].unsqueeze(2).to_broadcast([128, 16, 128]), op=ALU.mult),
            r=[B_cf, B_dts], w=[B_Lmat])
        yield
        for g in range(2):
            K.op("pe", lambda e, g=g: e.matmul(ps[2][:, g * 128:(g + 1) * 128], lhsT=BT[:, g, tok], rhs=CT[:, g, tok],
                                               start=True, stop=True),
                 r=[B_BT[g], B_CT[g]], w=[B_ps[2]])
        K.op("dve", lambda e: e.tensor_tensor(out=msc[:].rearrange("p (g l) -> p g l", g=2),
                                              in0=ps[2][:, 0:256].rearrange("p (g l) -> p g l", g=2),
                                              in1=CF(CI_LE).unsqueeze(1).to_broadcast([128, 2, 128]), op=ALU.mult),
             r=[B_ps[2], B_cf], w=[B_msc])
        yield

        def seg(r_):
            bank = 3 + r_ % 2
            for hh in range(4):
                h = r_ * 4 + hh
                K.op("pe", lambda e, hh=hh, h=h: e.matmul(
                    ps[bank][:, hh * 128:(hh + 1) * 128], lhsT=Lmat[:, h, :], rhs=CF(CI_LE), start=True, stop=True),
                    r=[B_Lmat, B_cf], w=[B_ps[bank]])
            d_, bd_ = dec[r_ % 2], B_dec[r_ % 2]
            K.op("act", lambda e: e.activation(out=d_[:], in_=ps[bank][:], func=AF.Exp), r=[B_ps[bank]], w=[bd_])

        seg(0)
        yield
        K.op("dve", lambda e: e.tensor_tensor(
            out=xw_[:].rearrange("p (h d) -> p h d", h=16), in0=xs_tok[:, c, :].rearrange("p (h d) -> p h d", h=16),
            in1=dts[:, 6, c, :].unsqueeze(2).to_broadcast([128, 16, 64]), op=ALU.mult),
            r=[B_xs[c], B_dts], w=[bxw])
        yield
        K.op("dve", lambda e: e.tensor_tensor(
            out=yA_[:].rearrange("p (h d) -> p h d", h=16), in0=xs_tok[:, c, :].rearrange("p (h d) -> p h d", h=16),
            in1=hv[:, 32:48].unsqueeze(2).to_broadcast([128, 16, 64]), op=ALU.mult),
            r=[B_xs[c], B_par], w=byA)
        yield
        for r_ in range(4):
            if r_ + 1 < 4:
                seg(r_ + 1)
                yield
            d_, bd_ = dec[r_ % 2], B_dec[r_ % 2]
            for hh in range(4):
                h = r_ * 4 + hh
                g = h // 8
                K.op("dve", lambda e, hh=hh, h=h, g=g, d_=d_: e.scalar_tensor_tensor(
                    out=attnT[:, h, :], in0=d_[:, hh * 128:(hh + 1) * 128], scalar=dts[:, 4, c, h:h + 1],
                    in1=msc[:, g * 128:(g + 1) * 128], op0=ALU.mult, op1=ALU.mult),
                    r=[bd_, B_dts, B_msc], w=[B_attn[h]])
                if hh % 2 == 1:
                    yield
            for hh in range(4):
                h = r_ * 4 + hh
                g = h // 8
                K.op("pe", lambda e, h=h, g=g: e.matmul(
                    ps[5 + g][:, (h % 8) * 64:(h % 8 + 1) * 64], lhsT=attnT[:, h, :], rhs=xs_tok[:, c, h * 64:(h + 1) * 64],
                    start=True, stop=True), r=[B_attn[h], B_xs[c]], w=[B_ps[5 + g]])
            yield
        for g in range(2):
            K.op("dve", lambda e, g=g: e.tensor_tensor(out=yA_[:, g * 512:(g + 1) * 512], in0=ps[5 + g][:],
                                                       in1=yA_[:, g * 512:(g + 1) * 512], op=ALU.add),
                 r=[B_ps[5 + g], byA[g]], w=[byA[g]])
            yield

    def back(c):
        tok = slice(c * 128, (c + 1) * 128)
        s_, bs_ = sz[c % 2], B_sz[c % 2]
        xw_, bxw = xw[c % 2], B_xw[c % 2]
        yA_, byA = yA[c % 2], B_yA[c % 2]
        if c > 0:
            for g in range(2):
                K.op("pe", lambda e, g=g: e.matmul(ps[7][:], lhsT=CT[:, g, tok], rhs=prevbf[:, g * 512:(g + 1) * 512],
                                                   start=True, stop=True),
                     r=[B_CT[g], B_prevbf[g]], w=[B_ps[7]])
                K.op("dve", lambda e, g=g: e.tensor_tensor(
                    out=yB[:, g * 512:(g + 1) * 512].rearrange("p (h d) -> p h d", h=8),
                    in0=ps[7][:].rearrange("p (h d) -> p h d", h=8),
                    in1=exa[:, c, g * 8:(g + 1) * 8].unsqueeze(2).to_broadcast([128, 8, 64]), op=ALU.mult),
                    r=[B_ps[7], B_exa], w=[B_yB[g]])
                yield
        if c < NT - 1:
            for g in range(2):
                bank = 1 if g == 0 else 7
                K.op("pe", lambda e, g=g, bank=bank: e.matmul(
                    ps[bank][:], lhsT=Btok[:, c, g * 128:(g + 1) * 128], rhs=xw_[:, g * 512:(g + 1) * 512], start=True, stop=True),
                    r=[B_Btok[c], bxw], w=[B_ps[bank]])
                pv = prev[:, g * 512:(g + 1) * 512]
                if c == 0:
                    K.op("dve", lambda e, pv=pv, bank=bank: e.tensor_copy(out=pv, in_=ps[bank][:]), r=[B_ps[bank]], w=[B_prev[g]])
                else:
                    K.op("dve", lambda e, pv=pv, g=g: e.tensor_tensor(
                        out=pv.rearrange("p (h d) -> p h d", h=8), in0=pv.rearrange("p (h d) -> p h d", h=8),
                        in1=exa[:, c, 32 + g * 8:32 + (g + 1) * 8].unsqueeze(2).to_broadcast([128, 8, 64]), op=ALU.mult),
                        r=[B_prev[g], B_exa], w=[B_prev[g]])
                    K.op("dve", lambda e, pv=pv, bank=bank: e.tensor_tensor(out=pv, in0=ps[bank][:], in1=pv, op=ALU.add),
                         r=[B_ps[bank], B_prev[g]], w=[B_prev[g]])
                K.op("act", lambda e, pv=pv, g=g: e.activation(out=prevbf[:, g * 512:(g + 1) * 512], in_=pv, func=AF.Copy),
                     r=[B_prev[g]], w=[B_prevbf[g]])
                yield
        for hf in range(2):
            proj_tok(wz, [B_wz[hf]], hf * 512, 512, c, ps[hf][:], hf)
            K.op("act", lambda e, hf=hf: e.activation(out=s_[:, hf * 512:(hf + 1) * 512], in_=ps[hf][:], func=AF.Silu),
                 r=[B_ps[hf]], w=[bs_[hf]])
        yield
        if c > 0:
            for g in range(2):
                hs = slice(g * 512, (g + 1) * 512)
                K.op("dve", lambda e, hs=hs: e.tensor_tensor(out=yA_[:, hs].rearrange("p (a b) -> p a b", a=2),
                                                             in0=yA_[:, hs].rearrange("p (a b) -> p a b", a=2),
                                                             in1=yB[:, hs].rearrange("p (a b) -> p a b", a=2), op=ALU.add),
                     r=[byA[g], B_yB[g]], w=[byA[g]])
                yield
        for g in range(2):
            hs = slice(g * 512, (g + 1) * 512)
            K.op("dve", lambda e, hs=hs: e.tensor_tensor(out=yA_[:, hs].rearrange("p (a b) -> p a b", a=2),
                                                         in0=yA_[:, hs].rearrange("p (a b) -> p a b", a=2),
                                                         in1=s_[:, hs].rearrange("p (a b) -> p a b", a=2), op=ALU.mult),
                 r=[byA[g], bs_[g]], w=[byA[g]])
            yield
        for g in range(2):
            K.op("act", lambda e, g=g: e.activation(out=yB[:, g * 512:(g + 1) * 512], in_=yA_[:, g * 512:(g + 1) * 512],
                                                    func=AF.Square, accum_out=sst[:, g:g + 1]),
                 r=[byA[g]], w=[B_yB[g], B_sst[g]])
        K.op("act", lambda e: e.activation(out=sst[:, 2:4], in_=sst[:, 0:2], func=AF.Ln, bias=EPS, scale=1.0 / 512),
             r=B_sst, w=[B_sst2])
        K.op("act", lambda e: e.activation(out=sst[:, 4:6], in_=sst[:, 2:4], func=AF.Exp, scale=-0.5), r=[B_sst2], w=[B_sst2])
        yield
        for g in range(2):
            K.op("dve", lambda e, g=g: e.scalar_tensor_tensor(
                out=gn[:, g * 512:(g + 1) * 512], in0=yA_[:, g * 512:(g + 1) * 512], scalar=sst[:, 4 + g:5 + g],
                in1=ssdn[:, g * 512:(g + 1) * 512], op0=ALU.mult, op1=ALU.mult),
                r=[byA[g], B_sst2, B_par], w=[B_gn[g]])
            yield
        for i in range(8):
            K.op("pe", lambda e, i=i: e.transpose(out=psbf(0)[:, i * 128:(i + 1) * 128], in_=gn[:, i * 128:(i + 1) * 128],
                                                  identity=CB(CI_ID)), r=[B_gn[i // 4], B_cb], w=[B_ps[0]])
        K.op("act", lambda e: e.activation(out=xs_tok[:, c, :], in_=psbf(0), func=AF.Copy), r=[B_ps[0]], w=[B_xs[c]])
        yield

    def interleave(gens, weights):
        active = [[g, w] for g, w in zip(gens, weights)]
        while active:
            for it in list(active):
                for _ in range(it[1]):
                    try:
                        next(it[0])
                    except StopIteration:
                        active.remove(it)
                        break

    interleave([front(0)], [1])
    for c in range(NT):
        gens = [back(c)]
        wts = [1]
        if c + 1 < NT:
            gens.append(front(c + 1))
            wts.append(2)
        interleave(gens, wts)

    ynT = xs_tok
    pre_w["ip0a"] = load_w([(C_Q, 256), (C_K, 256)])
    pre_w["ip0b"] = load_w([(C_V, 256)])
    K.fence()
    release(mC)
    if upto == "C":
        return finish([dump_bf("ynT", ynT[:], [128, NT, 1024], B_xs), ("dt", dts[:, 4, :, :].rearrange("p t h -> p (t h)"), [B_dts])])

    ogT = sb("ogT", [128, 8, L], BF16)
    B_og = [[[Buf() for _ in range(4)] for _ in range(2)] for _ in range(8)]
    mB = mark()
    qz = [sb("qz%d" % i, [128, 4, L], BF16) for i in range(2)]
    B_qz = [[[Buf() for _ in range(4)] for _ in range(4)] for _ in range(2)]
    B_qzero = Buf()
    kT = [sb("kT%d" % i, [128, 2, L], BF16) for i in range(2)]
    B_k = [[[Buf() for _ in range(4)] for _ in range(2)] for _ in range(2)]
    vtok = [sb("vtok%d" % i, [128, NT, 256], BF16) for i in range(2)]
    B_v = [[Buf() for _ in range(NT)] for _ in range(2)]
    et = [sb("et%d" % i, [128, 512]) for i in range(2)]
    B_et = [Buf(), Buf()]
    spb = [sb("spb%d" % i, [128, 512], BF16) for i in range(3)]
    B_sp = [Buf(), Buf(), Buf()]
    Ssum = [sb("Ssum%d" % i, [128, 512], BF16) for i in range(4)]
    B_S = [Buf() for _ in range(4)]
    wtb = [sb("wtb%d" % i, [128, 512], BF16) for i in range(3)]
    B_wt = [Buf(), Buf(), Buf()]
    uq = [0]
    gq = [0]
    NAB = 5
    for par in range(2):
        for hl4 in range(4):
            zr = slice(64, 128) if hl4 % 2 == 0 else slice(0, 64)
            K.op("pool", lambda e, par=par, hl4=hl4, zr=zr: e.memset(qz[par][zr, hl4, :], 0.0), w=[B_qzero])

    def inproj(qt):
        par = qt % 2
        if qt == 0:
            wt, bw = pre_w["ip0a"]
            wt2, bw2 = pre_w["ip0b"]
        else:
            wt, bw = load_w([(C_Q + qt * 256, 256), (C_K + qt * 256, 256)])
            wt2, bw2 = load_w([(C_V + qt * 256, 256)])
        yield
        rot = [0]

        def nb_():
            if qt != 0:
                return 5
            rot[0] += 1
            return 5 + rot[0] % 3
        for j in range(2):
            for g in range(4):
                bk = nb_()
                proj_feat(wt, bw, j * 128, g, bk)
                for a_ in range(2):
                    pr_ = slice(a_ * 64, a_ * 64 + 64)
                    hl_ = 2 * j + a_
                    K.op("dve", lambda e, hl_=hl_, pr_=pr_, g=g, bk=bk: e.tensor_scalar(
                        out=qz[par][pr_, hl_, g * 512:(g + 1) * 512], in0=ps[bk][pr_, :], scalar1=0.125, scalar2=None,
                        op0=ALU.mult), r=[B_ps[bk]], w=[B_qz[par][hl_][g]])
                yield
        for j in range(2):
            for g in range(4):
                bk = nb_()
                proj_feat(wt, bw, 256 + j * 128, g, bk)
                K.op("dve", lambda e, j=j, g=g, bk=bk: e.tensor_copy(out=kT[par][:, j, g * 512:(g + 1) * 512], in_=ps[bk][:]),
                     r=[B_ps[bk]], w=[B_k[par][j][g]])
                yield
        for t in range(NT):
            bk = nb_()
            proj_tok(wt2, bw2, 0, 256, t, ps[bk][:, 0:256], bk)
            K.op("dve", lambda e, t=t, bk=bk: e.tensor_copy(out=vtok[par][:, t, :], in_=ps[bk][:, 0:256]),
                 r=[B_ps[bk]], w=[B_v[par][t]])
            yield

    def attention(qt):
        par = qt % 2
        tasks = []
        for hl in range(4):
            for Q in range(4):
                u = uq[0]
                uq[0] += 1
                nblk = 4 * Q + 4
                q0p = None
                for idx in range(nblk):
                    c = nblk - 1 - idx
                    i = c - 4 * Q
                    q0 = max(i, 0) * 128
                    tasks.append(dict(hl=hl, Q=Q, u=u, idx=idx, c=c, i=i, q0=q0, q0p=q0p, gi=gq[0]))
                    gq[0] += 1
                    q0p = q0

        def stageA(T):
            gi, q0, q0p, idx, c, hl, Q = T["gi"], T["q0"], T["q0p"], T["idx"], T["c"], T["hl"], T["Q"]
            bA = gi % NAB
            j = hl // 2
            qs = Q * 512 + q0
            nq = 512 - q0
            K.op("pe", lambda e: e.matmul(ps[bA][:, q0:512], lhsT=kT[par][:, j, c * 128:(c + 1) * 128],
                                          rhs=qz[par][:, hl, qs:qs + nq], start=True, stop=False, skip_group_check=True),
                 r=[B_qz[par][hl][Q], B_qzero, B_k[par][j][c // 4]], w=[B_ps[bA]])
            if T["i"] >= 0:
                K.op("pe", lambda e: e.matmul(ps[bA][:, q0:q0 + 128], lhsT=CB(CI_ID), rhs=CB(CI_MASKM),
                                              start=False, stop=True, skip_group_check=True), r=[B_cb], w=[B_ps[bA]])
            e_, be = et[gi % 2], B_et[gi % 2]
            K.op("act", lambda e: e.activation(out=e_[:, q0:512], in_=ps[bA][:, q0:512], func=AF.Exp), r=[B_ps[bA]], w=[be])

        def stageA2(T):
            gi, q0, q0p, idx, c = T["gi"], T["q0"], T["q0p"], T["idx"], T["c"]
            e_, be = et[gi % 2], B_et[gi % 2]
            s__, bsp = spb[gi % 3], B_sp[gi % 3]
            K.op("act", lambda e: e.activation(out=s__[:, q0:512], in_=e_[:, q0:512], func=AF.Ln, bias=1.0, scale=1.0),
                 r=[be], w=[bsp])
            if c > 0:
                So, bSo = Ssum[gi % 4], B_S[gi % 4]
                Sn, bSn = Ssum[(gi + 1) % 4], B_S[(gi + 1) % 4]
                if idx == 0:
                    K.op("dve", lambda e: e.tensor_copy(out=Sn[:, q0:512], in_=s__[:, q0:512]), r=[bsp], w=[bSn])
                else:
                    K.op("dve", lambda e: e.tensor_tensor(out=Sn[:, q0p:512], in0=So[:, q0p:512], in1=s__[:, q0p:512], op=ALU.add),
                         r=[bSo, bsp], w=[bSn])
                    if q0 < q0p:
                        K.op("dve", lambda e: e.tensor_copy(out=Sn[:, q0:q0p], in_=s__[:, q0:q0p]), r=[bsp], w=[bSn])

        def stageB(T):
            gi, q0, q0p, idx = T["gi"], T["q0"], T["q0p"], T["idx"]
            bA = gi % NAB
            s__, bsp = spb[gi % 3], B_sp[gi % 3]
            K.op("pe", lambda e: e.matmul(ps[bA][:, q0:512], lhsT=CB(CI_NEGGE), rhs=s__[:, q0:512], start=False, stop=(idx == 0),
                                          skip_group_check=True), r=[bsp, B_cb], w=[B_ps[bA]])
            if idx > 0:
                So, bSo = Ssum[gi % 4], B_S[gi % 4]
                K.op("pe", lambda e: e.matmul(ps[bA][:, q0p:512], lhsT=CB(CI_NEGONES), rhs=So[:, q0p:512], start=False, stop=True,
                                              skip_group_check=True), r=[bSo, B_cb], w=[B_ps[bA]])
            w_, bw_ = wtb[gi % 3], B_wt[gi % 3]
            K.op("act", lambda e: e.activation(out=w_[:, q0:512], in_=ps[bA][:, q0:512], func=AF.Exp), r=[B_ps[bA]], w=[bw_])

        def stageC(T):
            gi, q0, idx, c, hl, Q, u = T["gi"], T["q0"], T["idx"], T["c"], T["hl"], T["Q"], T["u"]
            j = hl // 2
            hp = qt * 2 + j
            po = (hl % 2) * 64
            pr = slice(po, po + 64)
            bO = 6 + u % 2
            w_, bw_ = wtb[gi % 3], B_wt[gi % 3]
            K.op("pe", lambda e: e.matmul(ps[bO][:, q0:512], lhsT=vtok[par][:, c, j * 128:(j + 1) * 128], rhs=w_[:, q0:512],
                                          start=(idx == 0), stop=(c == 0), skip_group_check=True),
                 r=[B_v[par][c], bw_], w=[B_ps[bO]])
            if c == 0:
                K.op("dve", lambda e: e.tensor_copy(out=ogT[pr, hp, Q * 512:(Q + 1) * 512], in_=ps[bO][pr, :]),
                     r=[B_ps[bO]], w=[B_og[hp][hl % 2][Q]])

        NTK = len(tasks)
        LB, LC = 2, 3
        for s in range(NTK + LC):
            if s < NTK:
                stageA(tasks[s])
            if 0 <= s - LB < NTK:
                stageB(tasks[s - LB])
            if s < NTK:
                stageA2(tasks[s])
            if 0 <= s - LC < NTK:
                stageC(tasks[s - LC])
            yield

    def drain_(gen):
        if gen is not None:
            for _ in gen:
                pass

    drain_(inproj(0))
    for qt in range(4):
        bg = inproj(qt + 1) if qt + 1 < 4 else None
        n_ = 0
        for _ in attention(qt):
            n_ += 1
            if bg is not None and n_ % 3 == 0:
                try:
                    next(bg)
                except StopIteration:
                    bg = None
        drain_(bg)

    B_og_all = [B_og[hp][a][Q] for hp in range(8) for a in range(2) for Q in range(4)]
    pre_w["zsb0"] = load_w([(C_ZSB, 512)])
    pre_w["zsb1"] = load_w([(C_ZSB + 512, 512)])
    K.fence()
    release(mB)
    if upto == "B":
        return finish([dump_bf("ogT", ogT[:], [128, 8, L], B_og_all)])

    mergedT = sb("mergedT", [128, 8, L], BF16)
    B_mg = [Buf() for _ in range(NT)]
    wA = sb("wA", [128, 8, 1024], BF16)
    wB = sb("wB", [128, 8, 1024], BF16)
    B_wA = [Buf(), Buf()]
    B_wB = [Buf(), Buf()]
    wssd_v = wssd_d.rearrange("(c p) e -> p c e", p=128)
    wsb_v = wsb_d.rearrange("(c p) e -> p c e", p=128)
    wout_v = wout_d.rearrange("(c p) e -> p c e", p=128)
    for hf in range(2):
        K.dma("pool", wA[:, :, hf * 512:(hf + 1) * 512], wssd_v[:, :, hf * 512:(hf + 1) * 512], w=[B_wA[hf]])
        K.dma("pool", wB[:, :, hf * 512:(hf + 1) * 512], wsb_v[:, :, hf * 512:(hf + 1) * 512], w=[B_wB[hf]])
    mF1 = mark()
    szt = [sb("szt%d" % i, [128, 512], BF16) for i in range(2)]
    B_szt = [Buf(), Buf()]
    for i2 in range(2):
        wt, bw = pre_w["zsb%d" % i2]
        for j in range(4):
            hp = i2 * 4 + j
            for g in range(4):
                k_ = (j * 4 + g) % 2
                bank = 4 * k_
                proj_feat(wt, bw, j * 128, g, bank)
                K.op("act", lambda e, k_=k_, bank=bank: e.activation(out=szt[k_][:], in_=ps[bank][:], func=AF.Silu),
                     r=[B_ps[bank]], w=[B_szt[k_]])
                ogb_ = [B_og[hp][0][g], B_og[hp][1][g]]
                K.op("dve", lambda e, k_=k_, hp=hp, g=g: e.tensor_tensor(
                    out=ogT[:, hp, g * 512:(g + 1) * 512], in0=ogT[:, hp, g * 512:(g + 1) * 512], in1=szt[k_][:], op=ALU.mult),
                    r=ogb_ + [B_szt[k_]], w=ogb_)
    gs = [sb("gs%d" % i, [128, 512]) for i in range(2)]
    B_gs = [Buf(), Buf()]
    tt = [sb("tt%d" % i, [128, 512]) for i in range(2)]
    B_tt = [Buf(), Buf()]
    for d in range(8):
        wt, bw = load_w([(C_G + d * 128, 128), (C_G + 1024 + d * 128, 128)])
        if d == 5:
            K.dma("pool", wA[:, :, 0:512], wout_v[:, :, 0:512], w=[B_wA[0]])
        for g in range(4):
            st = (d * 4 + g) % 2
            ba, bb_, bg1, bg2 = 4 * st, 4 * st + 1, 4 * st + 2, 4 * st + 3
            yb = B_xs[4 * g:4 * g + 4]
            for c in range(8):
                K.op("pe", lambda e, c=c, ba=ba, d=d, g=g: e.matmul(
                    ps[ba][:], lhsT=wA[:, c, d * 128:(d + 1) * 128], rhs=ynT[:, 4 * g:4 * g + 4, c * 128:(c + 1) * 128],
                    start=(c == 0), stop=(c == 7)), r=[B_wA[d // 4]] + yb, w=[B_ps[ba]])
            ogb = [B_og[hp][a][g] for hp in range(8) for a in range(2)]
            for c in range(8):
                K.op("pe", lambda e, c=c, bb_=bb_, d=d, g=g: e.matmul(
                    ps[bb_][:], lhsT=wB[:, c, d * 128:(d + 1) * 128], rhs=ogT[:, c, g * 512:(g + 1) * 512],
                    start=(c == 0), stop=(c == 7)), r=[B_wB[d // 4]] + ogb, w=[B_ps[bb_]])
            proj_feat(wt, bw, 0, g, bg1)
            proj_feat(wt, bw, 128, g, bg2)
            for k_, (bg, bo) in enumerate(((bg1, ba), (bg2, bb_))):
                K.op("act", lambda e, k_=k_, bg=bg, d=d: e.activation(
                    out=gs[k_][:], in_=ps[bg][:], func=AF.Sigmoid, bias=bgate[:, k_ * 8 + d:k_ * 8 + d + 1], scale=1.0),
                    r=[B_ps[bg], B_par], w=[B_gs[k_]])
                K.op("dve", lambda e, k_=k_, bo=bo: e.tensor_tensor(out=tt[k_][:], in0=ps[bo][:], in1=gs[k_][:], op=ALU.mult),
                     r=[B_ps[bo], B_gs[k_]], w=[B_tt[k_]])
            K.op("dve", lambda e, d=d, g=g: e.tensor_tensor(out=mergedT[:, d, g * 512:(g + 1) * 512], in0=tt[0][:], in1=tt[1][:],
                                                            op=ALU.add), r=[B_tt[0], B_tt[1]], w=B_mg[4 * g:4 * g + 4])
    K.dma("pool", wA[:, :, 512:1024], wout_v[:, :, 512:1024], w=[B_wA[1]])
    K.fence()
    release(mF1)
    xb2 = [sb("xb2_%d" % i, [128, D]) for i in range(3)]
    B_xb2 = [Buf(), Buf(), Buf()]
    fin = [sb("fin0", [128, D])]
    B_fin = [Buf()]
    junk2 = sb("junk2", [128, 512], BF16)
    B_junk2 = Buf()
    st2 = sb("st2", [128, 4 * NT])
    B_st2 = [Buf() for _ in range(NT)]
    for t0_ in range(2):
        K.dma("sp", xb2[t0_][:], x_d[t0_ * 128:(t0_ + 1) * 128, :], w=[B_xb2[t0_]])
    def fin_mm(t):
        b0 = 2 * (t % 3)
        for hf in range(2):
            for c in range(8):
                K.op("pe", lambda e, c=c, hf=hf, bank=b0 + hf: e.matmul(
                    ps[bank][:], lhsT=mergedT[:, c, t * 128:(t + 1) * 128], rhs=wA[:, c, hf * 512:(hf + 1) * 512],
                    start=(c == 0), stop=(c == 7)), r=[B_wA[hf], B_mg[t]], w=[B_ps[b0 + hf]])

    def fin_stats(t):
        b0 = 2 * (t % 3)
        for hf in range(2):
            K.op("act", lambda e, hf=hf, bank=b0 + hf: e.activation(out=junk2[:], in_=ps[bank][:],
                                                                    func=AF.Square, accum_out=st2[:, 4 * t + hf:4 * t + hf + 1]),
                 r=[B_ps[b0 + hf]], w=[B_junk2, B_st2[t]])
        K.op("dve", lambda e: e.tensor_tensor(out=st2[:, 4 * t + 2:4 * t + 3], in0=st2[:, 4 * t:4 * t + 1],
                                              in1=st2[:, 4 * t + 1:4 * t + 2], op=ALU.add), r=[B_st2[t]], w=[B_st2[t]])
        K.op("act", lambda e: e.activation(out=st2[:, 4 * t + 3:4 * t + 4], in_=st2[:, 4 * t + 2:4 * t + 3], func=AF.Ln,
                                           bias=EPS, scale=1.0 / D), r=[B_st2[t]], w=[B_st2[t]])
        K.op("act", lambda e: e.activation(out=st2[:, 4 * t + 2:4 * t + 3], in_=st2[:, 4 * t + 3:4 * t + 4], func=AF.Exp,
                                           scale=-0.5), r=[B_st2[t]], w=[B_st2[t]])

    def fin_apply(t):
        xt, bx = xb2[t % 3], B_xb2[t % 3]
        if t + 2 < NT:
            K.dma("sp", xb2[(t + 2) % 3][:], x_d[(t + 2) * 128:(t + 3) * 128, :], w=[B_xb2[(t + 2) % 3]])
        f_, bf_ = fin[0], B_fin[0]
        b0 = 2 * (t % 3)
        for hf in range(2):
            K.op("dve", lambda e, hf=hf, bank=b0 + hf: e.scalar_tensor_tensor(
                out=f_[:, hf * 512:(hf + 1) * 512], in0=ps[bank][:], scalar=st2[:, 4 * t + 2:4 * t + 3],
                in1=npost[:, hf * 512:(hf + 1) * 512],
                op0=ALU.mult, op1=ALU.mult), r=[B_ps[b0 + hf], B_st2[t], B_par], w=[bf_])
        K.op("dve", lambda e: e.tensor_tensor(out=xt[:], in0=f_[:], in1=xt[:], op=ALU.add), r=[bf_, bx], w=[bx])
        out_ops.append(K.dma("sp", y_d[t * 128:(t + 1) * 128, :], xt[:], r=[bx]))

    fin_mm(0)
    fin_mm(1)
    fin_stats(0)
    for t in range(NT):
        if t + 2 < NT:
            fin_mm(t + 2)
        if t + 1 < NT:
            fin_stats(t + 1)
        fin_apply(t)
    K.wait_ops("sp", out_ops)
    K.emit()
    return nc


def prep_inputs(inputs):
    f = lambda a: np.ascontiguousarray(np.asarray(a, dtype=np.float32))
    x = f(inputs["x"])
    shared = {
        "w_in": f(inputs["w_in"][0]),
        "w_ssd_proj": f(inputs["w_ssd_proj"][0]),
        "w_sb_proj": f(inputs["w_sb_proj"][0]),
        "w_out": f(inputs["w_out"][0]),
        "cst": make_consts(),
        "npre_pc": f(np.asarray(inputs["norm_pre"][0]).reshape(8, 128).T),
        "npost_bc": f(np.broadcast_to(np.asarray(inputs["norm_post"][0])[None, :], (128, D))),
        "ssdn_bc": f(np.broadcast_to(np.asarray(inputs["ssd_norm"][0])[None, :], (128, D))),
        "bgate_pc": f(np.asarray(inputs["b_gate"][0]).reshape(16, 128).T),
        "convw_pc": f(np.asarray(inputs["conv_w"][0]).reshape(4, 12, 128).transpose(2, 1, 0).reshape(128, 48)),
        "convb_pc": f(np.asarray(inputs["conv_b"][0]).reshape(12, 128).T),
        "headvec_bc": f(np.broadcast_to(np.concatenate([np.asarray(inputs["dt_bias"][0]),
                                                         np.asarray(inputs["a_log"][0]),
                                                         np.asarray(inputs["d_skip"][0])])[None, :], (128, 48))),
    }
    in_maps = []
    for b in range(8):
        m = dict(shared)
        m["x"] = np.ascontiguousarray(x[b])
        in_maps.append(m)
    return in_maps


def kernel(**inputs):
    in_maps = prep_inputs(inputs)
    nc = build()
    res = run_bass_kernel_spmd(nc, in_maps, core_ids=list(range(8)))
    return np.stack([np.asarray(r["y"], dtype=np.float32) for r in res.results], axis=0)
```
